# Optimizing a Trainium2 kernel written in Bass

```python
import math
import jax, jax.numpy as jnp
from jax import lax
import numpy as np

D_MODEL = 1024
BATCH = 16
SEQ = 4096
DEPTH = 4

CHUNK = 64
N_MIXERS = 3
N_CONV_LAYERS = (DEPTH + 2) // 3
N_DIFF_LAYERS = (DEPTH + 1) // 3
N_POOL_LAYERS = DEPTH // 3
D_FF = -(-8 * D_MODEL // (3 * 256)) * 256
CONV_WIDTH = 3
DIFF_HEADS = 8
DIFF_HEAD_DIM = D_MODEL // (2 * DIFF_HEADS)
Q_BLOCK = 128
POOL_WINDOWS = (2, 4, 8, 16)
POOL_GROUPS = len(POOL_WINDOWS)
POOL_GROUP_DIM = D_MODEL // POOL_GROUPS
N_MOD = 6
EPS = 1e-6

kernel_name = 'hybrid_conv_diffattn_pool_stream_encoder'


def _rms(x, g):
    xf = x.astype(jnp.float32)
    y = xf * lax.rsqrt(jnp.mean(xf * xf, axis=-1, keepdims=True) + EPS)
    return (y * g.astype(jnp.float32)).astype(x.dtype)


def _alibi_slopes(n):
    return 2.0 ** (-8.0 * jnp.arange(1, n + 1, dtype=jnp.float32) / n)


def _conv_mixer(h, w_in, conv_w, w_out):
    z = h @ w_in
    bg, cg, xv = jnp.split(z, 3, axis=-1)
    u = cg * xv
    y = lax.conv_general_dilated(
        u, conv_w[:, None, :], window_strides=(1,), padding=[(CONV_WIDTH - 1, 0)],
        dimension_numbers=('NWC', 'WIO', 'NWC'), feature_group_count=D_MODEL)
    return (bg * y) @ w_out


def _diff_attention(h, w_qkv, q_norm, k_norm, lq1, lk1, lq2, lk2, subln, w_out, lam_init):
    b, s, _ = h.shape
    hd = DIFF_HEAD_DIM
    q, k, v = jnp.split(h @ w_qkv, 3, axis=-1)
    q = _rms(q.reshape(b, s, DIFF_HEADS, 2, hd), q_norm) * (hd ** -0.5)
    k = _rms(k.reshape(b, s, DIFF_HEADS, 2, hd), k_norm)
    v = v.reshape(b, s, DIFF_HEADS, 2 * hd)
    q = jnp.transpose(q, (0, 2, 3, 1, 4))
    k = jnp.transpose(k, (0, 2, 3, 1, 4))
    v = jnp.transpose(v, (0, 2, 1, 3))
    lam = (jnp.exp(jnp.sum(lq1.astype(jnp.float32) * lk1.astype(jnp.float32)))
           - jnp.exp(jnp.sum(lq2.astype(jnp.float32) * lk2.astype(jnp.float32))) + lam_init)
    slopes = _alibi_slopes(DIFF_HEADS)
    k_pos = jnp.arange(s)
    k_chunk = k_pos // CHUNK
    n_blk = s // Q_BLOCK
    q_blocks = jnp.moveaxis(q.reshape(b, DIFF_HEADS, 2, n_blk, Q_BLOCK, hd), 3, 0)

    def block(args):
        qb, blk = args
        q_pos = blk * Q_BLOCK + jnp.arange(Q_BLOCK)
        scores = jnp.einsum('bhmqd,bhmkd->bhmqk', qb, k).astype(jnp.float32)
        dist = jnp.abs(q_pos[:, None] - k_pos[None, :]).astype(jnp.float32)
        bias = -slopes[:, None, None] * dist
        allowed = k_chunk[None, :] <= (q_pos // CHUNK)[:, None]
        scores = jnp.where(allowed, scores + bias[None, :, None], -jnp.inf)
        p = jax.nn.softmax(scores, axis=-1)
        a = p[:, :, 0] - lam * p[:, :, 1]
        o = jnp.einsum('bhqk,bhke->bhqe', a.astype(v.dtype), v)
        return _rms(o, subln) * (1.0 - lam_init)

    out = lax.map(block, (q_blocks, jnp.arange(n_blk)))
    out = jnp.transpose(out, (1, 0, 3, 2, 4)).reshape(b, s, D_MODEL)
    return out @ w_out


def _pool_mixer(h, w_in, w_group, scale, w_out):
    b, s, _ = h.shape
    u = (h @ w_in).reshape(b, s, POOL_GROUPS, POOL_GROUP_DIM)
    uf = u.astype(jnp.float32)
    cs = jnp.pad(jnp.cumsum(uf, axis=1), ((0, 0), (1, 0), (0, 0), (0, 0)))
    hi = jnp.arange(s) + 1
    pooled = []
    for g, w in enumerate(POOL_WINDOWS):
        lo = jnp.maximum(hi - w, 0)
        cnt = jnp.minimum(hi, w).astype(jnp.float32)
        pooled.append((cs[:, hi, g] - cs[:, lo, g]) / cnt[None, :, None])
    pooled = jnp.stack(pooled, axis=2) - uf
    y = jnp.einsum('bsgc,gcd->bsgd', pooled.astype(h.dtype), w_group)
    y = y.reshape(b, s, D_MODEL) * scale
    return y @ w_out


def _swiglu(h, w1, w3, w2):
    return (jax.nn.silu(h @ w1) * (h @ w3)) @ w2


def setup_inputs(seed: int = 0) -> dict:
    key = jax.random.key(seed)
    ks = iter(jax.random.split(key, 40))
    f32 = jnp.float32
    D = D_MODEL

    def nrm(shape, fan_in, gain=1.0):
        return jax.random.normal(next(ks), shape, f32) * (gain * fan_in ** -0.5)

    def gain_vec(shape):
        return 1.0 + 0.1 * jax.random.normal(next(ks), shape, f32)

    def small(shape, s=0.02):
        return s * jax.random.normal(next(ks), shape, f32)

    return {
        'x': jax.random.normal(next(ks), (BATCH, SEQ, D), f32),
        'c': jax.random.normal(next(ks), (BATCH, D), f32),
        'ada_w': nrm((DEPTH, D, N_MOD * D), D, 0.5),
        'ada_b': small((DEPTH, N_MOD * D)),
        'norm_mix': gain_vec((DEPTH, D)),
        'norm_ffn': gain_vec((DEPTH, D)),
        'ffn_w1': nrm((DEPTH, D, D_FF), D),
        'ffn_w3': nrm((DEPTH, D, D_FF), D),
        'ffn_w2': nrm((DEPTH, D_FF, D), D_FF),
        'conv_w_in': nrm((N_CONV_LAYERS, D, 3 * D), D),
        'conv_w': nrm((N_CONV_LAYERS, CONV_WIDTH, D), CONV_WIDTH),
        'conv_w_out': nrm((N_CONV_LAYERS, D, D), D),
        'diff_w_qkv': nrm((N_DIFF_LAYERS, D, 3 * D), D),
        'diff_q_norm': gain_vec((N_DIFF_LAYERS, DIFF_HEAD_DIM)),
        'diff_k_norm': gain_vec((N_DIFF_LAYERS, DIFF_HEAD_DIM)),
        'diff_lq1': small((N_DIFF_LAYERS, DIFF_HEAD_DIM), 0.1),
        'diff_lk1': small((N_DIFF_LAYERS, DIFF_HEAD_DIM), 0.1),
        'diff_lq2': small((N_DIFF_LAYERS, DIFF_HEAD_DIM), 0.1),
        'diff_lk2': small((N_DIFF_LAYERS, DIFF_HEAD_DIM), 0.1),
        'diff_subln': gain_vec((N_DIFF_LAYERS, 2 * DIFF_HEAD_DIM)),
        'diff_w_out': nrm((N_DIFF_LAYERS, D, D), D),
        'pool_w_in': nrm((N_POOL_LAYERS, D, D), D),
        'pool_w_group': nrm((N_POOL_LAYERS, POOL_GROUPS, POOL_GROUP_DIM, POOL_GROUP_DIM), POOL_GROUP_DIM),
        'pool_scale': gain_vec((N_POOL_LAYERS, D)),
        'pool_w_out': nrm((N_POOL_LAYERS, D, D), D),
    }


def reference(x, c, ada_w, ada_b, norm_mix, norm_ffn, ffn_w1, ffn_w3, ffn_w2,
              conv_w_in, conv_w, conv_w_out,
              diff_w_qkv, diff_q_norm, diff_k_norm, diff_lq1, diff_lk1, diff_lq2, diff_lk2,
              diff_subln, diff_w_out,
              pool_w_in, pool_w_group, pool_scale, pool_w_out):
    cond = jax.nn.silu(c)
    for i in range(DEPTH):
        kind = i % N_MIXERS
        j = i // N_MIXERS
        mod = cond @ ada_w[i] + ada_b[i]
        sh1, sc1, g1, sh2, sc2, g2 = jnp.split(mod, N_MOD, axis=-1)
        h = _rms(x, norm_mix[i]) * (1.0 + sc1[:, None]) + sh1[:, None]
        if kind == 0:
            y = _conv_mixer(h, conv_w_in[j], conv_w[j], conv_w_out[j])
        elif kind == 1:
            lam_init = 0.8 - 0.6 * math.exp(-0.3 * i)
            y = _diff_attention(h, diff_w_qkv[j], diff_q_norm[j], diff_k_norm[j],
                                diff_lq1[j], diff_lk1[j], diff_lq2[j], diff_lk2[j],
                                diff_subln[j], diff_w_out[j], lam_init)
        else:
            y = _pool_mixer(h, pool_w_in[j], pool_w_group[j], pool_scale[j], pool_w_out[j])
        x = x + g1[:, None] * y
        h = _rms(x, norm_ffn[i]) * (1.0 + sc2[:, None]) + sh2[:, None]
        x = x + g2[:, None] * _swiglu(h, ffn_w1[i], ffn_w3[i], ffn_w2[i])
    return x
```

```python
import math
from contextlib import ExitStack
import numpy as np
import concourse.bass as bass
import concourse.mybir as mybir
from concourse.bass_utils import run_bass_kernel_spmd

F32 = mybir.dt.float32
BF16 = mybir.dt.bfloat16
AF = mybir.ActivationFunctionType
ALU = mybir.AluOpType

ENGS = ("pe", "act", "dve", "pool", "sp")
D = 1024
SEQ = 4096
NSEQ = 2
NTOK = NSEQ * SEQ
T = 512
NT = NTOK // T
TPS = SEQ // T
DFF = 2816
HF = DFF // 2
NFC = HF // 128
DEPTH = 4
EPS = 1e-6
WREG = 33792
NCORES = 8


class Res:
    __slots__ = ("name", "last_w", "readers")

    def __init__(self, name=""):
        self.name = name
        self.last_w = None
        self.readers = {}


class DSem:
    __slots__ = ("key", "count")

    def __init__(self, key):
        self.key = key
        self.count = 0


class Sched:
    def __init__(self, nc):
        self.nc = nc
        self.ops = {e: [] for e in ENGS}
        self.cnt = {e: 0 for e in ENGS}
        self.seen = {e: {} for e in ENGS}
        self.dsems = []

    def dsem(self):
        d = DSem("d%d" % len(self.dsems))
        self.dsems.append(d)
        return d

    def _deps(self, eng, reads, writes):
        deps = {}
        skip = "pe" if eng == "pe" else None
        for r in reads:
            t = r.last_w
            if t is not None and t[0] != skip and deps.get(t[0], 0) < t[1]:
                deps[t[0]] = t[1]
        for w in writes:
            t = w.last_w
            if t is not None and t[0] != skip and deps.get(t[0], 0) < t[1]:
                deps[t[0]] = t[1]
            for k, v in w.readers.items():
                if k != skip and deps.get(k, 0) < v:
                    deps[k] = v
        seen = self.seen[eng]
        for k, v in deps.items():
            if seen.get(k, 0) < v:
                seen[k] = v
                self.ops[eng].append(("wait", k, v))

    def _mark(self, tok, reads, writes):
        k, v = tok
        for r in reads:
            if r.readers.get(k, 0) < v:
                r.readers[k] = v
        for w in writes:
            w.last_w = tok
            w.readers = {}

    def op(self, eng, fn, reads=(), writes=()):
        self._deps(eng, reads, writes)
        self.cnt[eng] += 1
        tok = (eng, self.cnt[eng])
        self.ops[eng].append(("ins", fn))
        self._mark(tok, reads, writes)
        return tok

    def dma(self, eng, fn, ds, reads=(), writes=()):
        self._deps(eng, reads, writes)
        ds.count += 16
        tok = (ds.key, ds.count)
        self.ops[eng].append(("dma", fn, ds.key))
        self._mark(tok, reads, writes)
        return tok

    def wait_tok(self, eng, tok):
        k, v = tok
        if self.seen[eng].get(k, 0) < v:
            self.seen[eng][k] = v
            self.ops[eng].append(("wait", k, v))

    def emit(self):
        nc = self.nc
        with ExitStack() as st:
            sems = {}
            for e in ENGS:
                sems[e] = st.enter_context(nc.semaphore("s_" + e))
            for d in self.dsems:
                sems[d.key] = st.enter_context(nc.semaphore("s_" + d.key))
            block = st.enter_context(nc.Block())

            def run(ename):
                def body(eng):
                    for item in self.ops[ename]:
                        if item[0] == "wait":
                            eng.wait_ge(sems[item[1]], item[2])
                        elif item[0] == "ins":
                            item[1](eng).then_inc(sems[ename], 1)
                        else:
                            item[1](eng).then_inc(sems[item[2]], 16)
                return body

            block.tensor(run("pe"))
            block.scalar(run("act"))
            block.vector(run("dve"))
            block.gpsimd(run("pool"))
            block.sync(run("sp"))


class Cols:
    def __init__(self):
        self.off = {}
        self.n = 0

    def add(self, name, n):
        self.off[name] = self.n
        self.n += n

    def __getitem__(self, name):
        return self.off[name]


def col_layout():
    c = Cols()
    c.add("adab", DEPTH * 48)
    c.add("nmix", DEPTH * 8)
    c.add("nffn", DEPTH * 8)
    c.add("convw", 2 * 3 * 8)
    c.add("pscale", 8)
    c.add("qn", 1)
    c.add("kn", 1)
    c.add("subln", 1)
    c.add("lam", 4)
    c.add("invc", 64)
    c.add("abias", NAB)
    return c


def att_centers(h):
    if h == 0:
        return [(i * 256, (i + 1) * 256, i * 256 + 128) for i in range(2)]
    return [(0, 512, 256)]


def att_bias_index():
    idx = {}
    n = 0
    for h in range(8):
        for qt in range(TPS):
            for kb in range(4 * qt + 4):
                d0 = 512 * qt - 128 * kb
                for (_, _, cq) in att_centers(h):
                    key = (h, d0 + cq)
                    if key not in idx:
                        idx[key] = n
                        n += 1
    return idx, n


ABI, NAB = att_bias_index()
CL = col_layout()


def vec_cols(v):
    v = np.asarray(v, np.float32)
    return np.ascontiguousarray(v.reshape(-1, 128).T)


class Builder:
    def __init__(self, layers=(0, 1, 2, 3), stop_after=None):
        self.layers = layers
        self.stop_after = stop_after
        self.nc = nc = bass.Bass("TRN2", target_bir_lowering=False)
        self.S = Sched(nc)
        dt = nc.dram_tensor
        self.xin = dt("xin", [D, NTOK], F32, kind="ExternalInput").ap()
        self.out = dt("out", [D, NTOK], F32, kind="ExternalOutput").ap()
        self.ccol = dt("ccol", [128, 16], F32, kind="ExternalInput").ap()
        self.cols = dt("cols", [128, CL.n], F32, kind="ExternalInput").ap()
        self.ada_w = dt("ada_w", [DEPTH, D, 6 * D], F32, kind="ExternalInput").ap()
        self.ffn_w1 = dt("ffn_w1", [DEPTH, D, DFF], F32, kind="ExternalInput").ap()
        self.ffn_w3 = dt("ffn_w3", [DEPTH, D, DFF], F32, kind="ExternalInput").ap()
        self.ffn_w2 = dt("ffn_w2", [DEPTH, DFF, D], F32, kind="ExternalInput").ap()
        self.conv_w_in = dt("conv_w_in", [2, D, 3 * D], F32, kind="ExternalInput").ap()
        self.conv_w_out = dt("conv_w_out", [2, D, D], F32, kind="ExternalInput").ap()
        self.diff_w_qkv = dt("diff_w_qkv", [1, D, 3 * D], F32, kind="ExternalInput").ap()
        self.diff_w_out = dt("diff_w_out", [1, D, D], F32, kind="ExternalInput").ap()
        self.pool_w_in = dt("pool_w_in", [1, D, D], F32, kind="ExternalInput").ap()
        self.pool_w_group = dt("pool_w_group", [1, 4, 256, 256], F32, kind="ExternalInput").ap()
        self.pool_w_out = dt("pool_w_out", [1, D, D], F32, kind="ExternalInput").ap()
        self.amask = dt("amask", [128, 32 * 512], BF16, kind="ExternalInput").ap()
        self.xa = dt("xa", [D, NTOK], F32).ap()
        self.xb = dt("xb", [D, NTOK], F32).ap()
        self.hscr = dt("hscr", [D, NTOK], BF16).ap()
        self.qscr = dt("qscr", [D, NTOK], BF16).ap()
        self.kscr = dt("kscr", [D, NTOK], BF16).ap()
        self.vscr = dt("vscr", [NTOK, D], BF16).ap()
        self.oscr = dt("oscr", [D, NTOK], BF16).ap()
        self.r_dram = {}

    def dres(self, name, t):
        key = (name, t)
        if key not in self.r_dram:
            self.r_dram[key] = Res("%s%d" % key)
        return self.r_dram[key]

    def build(self):
        nc, S = self.nc, self.S
        with ExitStack() as st:
            sb = lambda name, shape, dtp: st.enter_context(nc.sbuf_tensor(name, shape, dtp))
            self.W = [sb("WA", [128, WREG], BF16), sb("WB", [128, WREG], BF16)]
            self.rW = [[Res("W%d_%d" % (i, j)) for j in range(4)] for i in range(2)]
            self.dW = [[S.dsem() for j in range(4)] for i in range(2)]
            self.rG = [Res("G0"), Res("G1")]
            self.xs = [sb("xs%d" % i, [128, 8, T], F32) for i in range(2)]
            self.r_xs = [[Res("xs%d_%d" % (i, m)) for m in range(8)] for i in range(2)]
            self.d_xs = [S.dsem() for i in range(2)]
            self.d_st = [S.dsem() for i in range(2)]
            self.hs = sb("hs", [128, 8, T], BF16)
            self.r_hs = [Res("hs%d" % m) for m in range(8)]
            self.d_hs = S.dsem()
            self.d_hst = S.dsem()
            self.g = sb("g", [128, NFC, T], BF16)
            self.r_g = [Res("g%d" % m) for m in range(NFC)]
            self.sq = [sb("sq%d" % i, [128, T], BF16) for i in range(2)]
            self.r_sq = [Res("sq%d" % i) for i in range(2)]
            self.sd = sb("sd", [128, T], F32)
            self.r_sd = Res("sd")
            self.rstd = sb("rstd", [128, T], F32)
            self.r_rstd = Res("rstd")
            self.tmpall = sb("tmpall", [128, 6, 528], F32)
            self.tmp = [self.tmpall[:, i, :] for i in range(6)]
            self.r_tmp = [Res("tmp%d" % i) for i in range(6)]
            self.halo = sb("halo", [128, 8, 16], F32)
            self.r_halo = [Res("halo%d" % m) for m in range(8)]
            self.colt = sb("colt", [128, CL.n], F32)
            self.r_colt = Res("colt")
            self.modT = sb("modT", [128, DEPTH * 48, 2], F32)
            self.r_mod = Res("mod")
            self.Acol = sb("Acol", [128, DEPTH * 2 * 8, 2], F32)
            self.r_A = Res("A")
            self.cond = sb("cond", [128, 16], BF16)
            self.r_cond = Res("cond")
            self.csb = sb("csb", [128, 16], F32)
            self.ones = sb("ones", [128, 128], BF16)
            self.r_ones = Res("ones")
            self.bones = sb("bones", [128, 128], BF16)
            self.epsc = sb("epsc", [128, 1], F32)
            self.lamc = sb("lamc", [128, 8], F32)
            self.onesf = sb("onesf", [128, 128], F32)
            self.r_lam = Res("lam")
            self.psall = st.enter_context(nc.psum_tensor("psall", [128, 8, T], F32))
            self.ps = [self.psall[:, i, :] for i in range(8)]
            self.r_ps = [Res("ps%d" % i) for i in range(8)]
            self.d_misc = S.dsem()

            self.prologue()
            src = self.xin
            passes = []
            for l in self.layers:
                kind = l % 3
                if kind == 0:
                    passes.append(("conv", l))
                elif kind == 1:
                    passes.append(("qkv", l))
                    passes.append(("att", l))
                    passes.append(("oproj", l))
                else:
                    passes.append(("pool", l))
                if self.stop_after == ("mix", l):
                    break
                passes.append(("ffn", l, 0))
                passes.append(("ffn", l, 1))
            self.passes = passes
            self.wreg_of = {}
            reg = 0
            for i, p in enumerate(passes):
                if p[0] == "att":
                    self.wreg_of[i] = None
                    continue
                self.wreg_of[i] = reg
                reg ^= 1
            self.loaded = set()
            self.load_weights(0)
            scr = [self.xa, self.xb]
            si = 0
            last_tok = []
            for i, p in enumerate(passes):
                nxt = i + 1
                while nxt < len(passes) and passes[nxt][0] == "att":
                    nxt += 1
                if nxt < len(passes) and p[0] != "att":
                    self.load_weights(nxt)
                is_last = (i == len(passes) - 1)
                reads_x = p[0] not in ("att",)
                writes_x = p[0] not in ("att", "qkv")
                dst = None
                if writes_x:
                    dst = self.out if is_last else scr[si]
                if p[0] == "conv":
                    toks = self.pass_conv(i, p[1], src, dst)
                elif p[0] == "pool":
                    toks = self.pass_pool(i, p[1], src, dst)
                elif p[0] == "ffn":
                    toks = self.pass_ffn(i, p[1], p[2], src, dst)
                elif p[0] == "qkv":
                    toks = self.pass_qkv(i, p[1], src)
                elif p[0] == "att":
                    toks = self.pass_att(i, p[1])
                elif p[0] == "oproj":
                    toks = self.pass_oproj(i, p[1], src, dst)
                if writes_x:
                    src = dst
                    si ^= 1
                    last_tok = toks
            for ds in self.d_st:
                S.wait_tok("sp", (ds.key, ds.count))
            S.emit()
        return nc

    def wview(self, reg, off, k, n):
        return self.W[reg][:, off:off + k * n].rearrange("p (k n) -> p k n", k=k)

    def load_w(self, dst3, src2, res, ds, guard=None):
        S = self.S
        wr = [res] if guard is None else [res, guard]
        sv = src2.rearrange("(k p) n -> p k n", p=128)
        K, N = dst3.shape[1], dst3.shape[2]
        kstep = 4
        for k0 in range(0, K, kstep):
            k1 = min(K, k0 + kstep)
            for c0 in range(0, N, 1024):
                c1 = min(N, c0 + 1024)
                S.dma("pool", lambda e, k0=k0, k1=k1, c0=c0, c1=c1: e.dma_start(
                    out=dst3[:, k0:k1, c0:c1], in_=sv[:, k0:k1, c0:c1]), ds, writes=wr)
                wr = [res]

    def load_weights(self, i):
        if i in self.loaded:
            return
        self.loaded.add(i)
        p = self.passes[i]
        reg = self.wreg_of[i]
        if reg is None:
            return
        rW, dW = self.rW[reg], self.dW[reg]
        if p[0] == "ffn":
            l, hf = p[1], p[2]
            self.load_w(self.wview(reg, 0, 8, HF), self.ffn_w1[l][:, hf * HF:(hf + 1) * HF], rW[0], dW[0], self.rG[reg])
            self.load_w(self.wview(reg, 8 * HF, 8, HF), self.ffn_w3[l][:, hf * HF:(hf + 1) * HF], rW[1], dW[1])
            self.load_w(self.wview(reg, 16 * HF, NFC, D), self.ffn_w2[l][hf * HF:(hf + 1) * HF, :], rW[2], dW[2])
        elif p[0] == "conv":
            j = p[1] // 3
            self.load_w(self.wview(reg, 0, 8, 3 * D), self.conv_w_in[j], rW[0], dW[0], self.rG[reg])
            self.load_w(self.wview(reg, 24 * D, 8, D), self.conv_w_out[j], rW[1], dW[1])
        elif p[0] == "pool":
            self.load_w(self.wview(reg, 0, 8, D), self.pool_w_in[0], rW[0], dW[0], self.rG[reg])
            for gi in range(4):
                self.load_w(self.W[reg][:, 8 * D + gi * 512:8 * D + (gi + 1) * 512].rearrange("p (k n) -> p k n", k=2),
                            self.pool_w_group[0][gi], rW[1], dW[1])
            self.load_w(self.wview(reg, 8 * D + 2048, 8, D), self.pool_w_out[0], rW[2], dW[2])
        elif p[0] == "qkv":
            self.load_w(self.wview(reg, 0, 8, 3 * D), self.diff_w_qkv[0], rW[0], dW[0], self.rG[reg])
        elif p[0] == "oproj":
            self.load_w(self.wview(reg, 0, 8, D), self.diff_w_out[0], rW[0], dW[0], self.rG[reg])

    def col(self, name, idx=0):
        c = CL[name] + idx
        return self.colt[:, c:c + 1]

    def modc(self, l, j, m, b):
        return self.modT[:, l * 48 + j * 8 + m, b:b + 1]

    def Ac(self, l, which, m, b):
        return self.Acol[:, (l * 2 + which) * 8 + m, b:b + 1]

    def prologue(self):
        nc, S = self.nc, self.S
        r_c = Res("c")
        S.dma("sp", lambda e: e.dma_start(out=self.colt[:], in_=self.cols), self.d_misc, writes=[self.r_colt])
        S.dma("sp", lambda e: e.dma_start(out=self.csb[:], in_=self.ccol), S.dsem(), writes=[r_c])
        S.op("dve", lambda e: e.memset(self.ones[:], 1.0), writes=[self.r_ones])
        S.op("dve", lambda e: e.memset(self.epsc[:], EPS), writes=[self.r_ones])
        S.op("dve", lambda e: e.memset(self.onesf[:], 1.0), writes=[self.r_ones])
        S.op("dve", lambda e: e.memset(self.lamc[:], 64.0 * EPS), writes=[self.r_lam])
        S.op("dve", lambda e: e.memset(self.bones[:], 0.0), writes=[self.r_ones])
        S.op("dve", lambda e: e.memset(self.bones[0:64, 0:64], 1.0), writes=[self.r_ones])
        S.op("dve", lambda e: e.memset(self.bones[64:128, 64:128], 1.0), writes=[self.r_ones])
        S.op("act", lambda e: e.activation(out=self.cond[:], in_=self.csb[:], func=AF.Silu),
             reads=[r_c], writes=[self.r_cond])
        condv = self.cond[:].rearrange("p (k b) -> p k b", b=2)
        psm = self.ps[7][:, 0:DEPTH * 96].rearrange("p (c b) -> p c b", b=2)
        slots = []
        for reg in range(2):
            for q in range(4):
                slots.append((self.wview(reg, q * 8 * D, 8, D), self.rW[reg][q], self.dW[reg][q], self.rG[reg]))
        has_qkv = any(l % 3 == 1 for l in self.layers)
        now = [l for i, l in enumerate(self.layers) if (i < 2 or not has_qkv)]
        self.deferred_mods = [l for l in self.layers if l not in now]
        self.psm = psm
        self.condv = condv
        n = 0
        for l in now:
            for j in range(6):
                self.emit_mod_part(l, j, slots[n % len(slots)])
                n += 1
        self.emit_mod_finish(now, first=True)

    def emit_mod_part(self, l, j, slot):
        S = self.S
        stg, rs, ds, rg = slot
        psm, condv = self.psm, self.condv
        self.load_w(stg, self.ada_w[l][:, j * D:(j + 1) * D], rs, ds)

        def mm(e):
            for m in range(8):
                cidx = l * 48 + j * 8 + m
                for k in range(8):
                    ins = e.matmul(psm[:, cidx, :], stg[:, k, m * 128:(m + 1) * 128], condv[:, k, :],
                                   start=(k == 0), stop=(k == 7))
            return ins
        S.op("pe", mm, reads=[rs, rg, self.r_cond], writes=[self.r_ps[7]])

    def emit_mod_finish(self, layers, first=False):
        S = self.S
        psm = self.psm
        if not layers:
            return
        c0 = min(layers) * 48
        c1 = (max(layers) + 1) * 48
        if first:
            c0, c1 = 0, DEPTH * 48
        adab = self.colt[:, CL["adab"] + c0:CL["adab"] + c1]
        for b in range(2):
            S.op("dve", lambda e, b=b: e.tensor_tensor(out=self.modT[:, c0:c1, b], in0=psm[:, c0:c1, b], in1=adab, op=ALU.add),
                 reads=[self.r_ps[7], self.r_colt], writes=[self.r_mod])
        for l in layers:
            for which in range(2):
                gname = "nmix" if which == 0 else "nffn"
                gcols = self.colt[:, CL[gname] + l * 8:CL[gname] + l * 8 + 8]
                jsc = 1 if which == 0 else 4
                for b in range(2):
                    S.op("dve", lambda e, l=l, which=which, b=b, gcols=gcols, jsc=jsc: e.scalar_tensor_tensor(
                        out=self.Acol[:, (l * 2 + which) * 8:(l * 2 + which) * 8 + 8, b],
                        in0=self.modT[:, l * 48 + jsc * 8:l * 48 + jsc * 8 + 8, b], scalar=1.0, in1=gcols,
                        op0=ALU.add, op1=ALU.mult), reads=[self.r_mod, self.r_colt], writes=[self.r_A])

    def xcols(self, t):
        return slice(t * T, (t + 1) * T)

    def load_x(self, src, t, srcname):
        S = self.S
        slot = t % 2
        sv = src.rearrange("(m p) n -> p m n", p=128)
        reads = [] if src is self.xin else [self.dres(id(src), t)]
        for m0 in (0, 4):
            S.dma("sp", lambda e, m0=m0: e.dma_start(out=self.xs[slot][:, m0:m0 + 4, :],
                                                     in_=sv[:, m0:m0 + 4, self.xcols(t)]),
                  self.d_xs[slot], reads=reads, writes=self.r_xs[slot][m0:m0 + 4])

    def store_x(self, dst, t):
        S = self.S
        slot = t % 2
        dv = dst.rearrange("(m p) n -> p m n", p=128)
        toks = []
        for m0 in (0, 4):
            toks.append(S.dma("sp", lambda e, m0=m0: e.dma_start(out=dv[:, m0:m0 + 4, self.xcols(t)],
                                                                 in_=self.xs[slot][:, m0:m0 + 4, :]),
                              self.d_st[slot], reads=self.r_xs[slot][m0:m0 + 4], writes=[self.dres(id(dst), t)]))
        return toks

    def emit_norm(self, t, l, which, ring=(0, 1)):
        self.emit_norm_stats(t, l, which)
        for m in range(8):
            self.emit_norm_chunk(t, l, which, m, ring)

    def emit_norm_stats(self, t, l, which):
        S = self.S
        slot = t % 2
        xs, rx = self.xs[slot], self.r_xs[slot]
        pst, rpst = self.ps[6], self.r_ps[6]
        for m in range(8):
            q = m % 2
            S.op("act", lambda e, m=m, q=q: e.activation(out=self.sq[q][:], in_=xs[:, m, :], func=AF.Square),
                 reads=[rx[m]], writes=[self.r_sq[q]])
            S.op("pe", lambda e, m=m, q=q: e.matmul(pst[:], self.ones[:], self.sq[q][:], start=(m == 0), stop=(m == 7)),
                 reads=[self.r_sq[q], self.r_ones], writes=[rpst])
        S.op("act", lambda e: e.activation(out=self.rstd[:], in_=pst[:], func=AF.Ln, bias=self.epsc[:], scale=1.0 / D),
             reads=[rpst, self.r_ones], writes=[self.r_rstd])
        S.op("act", lambda e: e.activation(out=self.rstd[:], in_=self.rstd[:], func=AF.Exp, scale=-0.5),
             reads=[self.r_rstd], writes=[self.r_rstd])

    def emit_norm_chunk(self, t, l, which, m, ring=(0, 1)):
        S = self.S
        slot = t % 2
        b = t // TPS
        xs, rx = self.xs[slot], self.r_xs[slot]
        jsh = 0 if which == 0 else 3
        q = ring[m % len(ring)]
        tmp, rt = self.tmp[q], self.r_tmp[q]
        S.op("dve", lambda e: e.scalar_tensor_tensor(
            out=tmp[:, 0:T], in0=xs[:, m, :], scalar=self.Ac(l, which, m, b), in1=self.rstd[:],
            op0=ALU.mult, op1=ALU.mult), reads=[rx[m], self.r_rstd, self.r_A], writes=[rt])
        S.op("act", lambda e: e.activation(
            out=self.hs[:, m, :], in_=tmp[:, 0:T], func=AF.Identity, bias=self.modc(l, jsh, m, b), scale=1.0),
            reads=[rt, self.r_mod], writes=[self.r_hs[m]])

    def emit_oproj(self, t, l, wout, r_wout, vin, r_vin, nk, jg, psbanks, hook=None):
        S = self.S
        slot = t % 2
        b = t // TPS
        xs, rx = self.xs[slot], self.r_xs[slot]
        for m2 in range(8):
            pb = psbanks[m2 % len(psbanks)]

            def mm(e, m2=m2, pb=pb):
                for k in range(nk):
                    ins = e.matmul(self.ps[pb][:], wout[:, k, m2 * 128:(m2 + 1) * 128], vin[:, k, :],
                                   start=(k == 0), stop=(k == nk - 1))
                return ins
            S.op("pe", mm, reads=list(r_wout) + list(r_vin), writes=[self.r_ps[pb]])
            S.op("dve", lambda e, m2=m2, pb=pb: e.scalar_tensor_tensor(
                out=xs[:, m2, :], in0=self.ps[pb][:], scalar=self.modc(l, jg, m2, b), in1=xs[:, m2, :],
                op0=ALU.mult, op1=ALU.add), reads=[self.r_ps[pb], rx[m2], self.r_mod], writes=[rx[m2]])
            if hook is not None:
                hook(m2)

    def pass_ffn(self, pi, l, hf, src, dst):
        S = self.S
        reg = self.wreg_of[pi]
        w1 = self.wview(reg, 0, 8, HF)
        w3 = self.wview(reg, 8 * HF, 8, HF)
        w2 = self.wview(reg, 16 * HF, NFC, D)
        rW = [[r, self.rG[reg]] for r in self.rW[reg]]
        hv = self.hscr.rearrange("(m p) n -> p m n", p=128)
        toks = []

        def get_h(t):
            if hf == 0:
                self.emit_norm(t, l, 1)
                S.dma("sp", lambda e, t=t: e.dma_start(out=hv[:, :, self.xcols(t)], in_=self.hs[:]),
                      self.d_hst, reads=self.r_hs, writes=[self.dres("h", t)])
            else:
                S.dma("sp", lambda e, t=t: e.dma_start(out=self.hs[:], in_=hv[:, :, self.xcols(t)]),
                      self.d_hs, reads=[self.dres("h", t)], writes=self.r_hs)

        self.load_x(src, 0, None)
        get_h(0)
        for t in range(NT):
            if t + 1 < NT:
                self.load_x(src, t + 1, None)
            for f in range(NFC):
                pa, pb = f % 2, 2 + f % 2

                def mm1(e, f=f, pa=pa):
                    for k in range(8):
                        ins = e.matmul(self.ps[pa][:], w1[:, k, f * 128:(f + 1) * 128], self.hs[:, k, :],
                                       start=(k == 0), stop=(k == 7))
                    return ins

                def mm3(e, f=f, pb=pb):
                    for k in range(8):
                        ins = e.matmul(self.ps[pb][:], w3[:, k, f * 128:(f + 1) * 128], self.hs[:, k, :],
                                       start=(k == 0), stop=(k == 7))
                    return ins
                S.op("pe", mm1, reads=rW[0] + self.r_hs, writes=[self.r_ps[pa]])
                S.op("pe", mm3, reads=rW[1] + self.r_hs, writes=[self.r_ps[pb]])
                q = 2 + f % 2
                S.op("act", lambda e, pa=pa, q=q: e.activation(out=self.tmp[q][:, 0:T], in_=self.ps[pa][:], func=AF.Silu),
                     reads=[self.r_ps[pa]], writes=[self.r_tmp[q]])
                S.op("dve", lambda e, f=f, pb=pb, q=q: e.tensor_tensor(out=self.g[:, f, :], in0=self.ps[pb][:],
                                                                        in1=self.tmp[q][:, 0:T], op=ALU.mult),
                     reads=[self.r_ps[pb], self.r_tmp[q]], writes=[self.r_g[f]])
            hook = None
            if t + 1 < NT:
                if hf == 0:
                    self.emit_norm_stats(t + 1, l, 1)

                    def hook(m2, t=t):
                        if m2 < 4:
                            self.emit_norm_chunk(t + 1, l, 1, 2 * m2, (0, 1, 4, 5))
                            self.emit_norm_chunk(t + 1, l, 1, 2 * m2 + 1, (0, 1, 4, 5))
                        if m2 == 4:
                            S.dma("sp", lambda e: e.dma_start(out=hv[:, :, self.xcols(t + 1)], in_=self.hs[:]),
                                  self.d_hst, reads=self.r_hs, writes=[self.dres("h", t + 1)])
                else:
                    get_h(t + 1)
            self.emit_oproj(t, l, w2, rW[2], self.g, self.r_g, NFC, 5, (4, 5), hook)
            toks = self.store_x(dst, t)
        return toks

    def pass_conv(self, pi, l, src, dst):
        S = self.S
        reg = self.wreg_of[pi]
        j = l // 3
        w_in = self.wview(reg, 0, 8, 3 * D)
        w_out = self.wview(reg, 24 * D, 8, D)
        rW = [[r, self.rG[reg]] for r in self.rW[reg]]
        v, r_v = self.g, self.r_g
        toks = []
        self.load_x(src, 0, None)
        self.emit_norm(0, l, 0)
        for t in range(NT):
            if t + 1 < NT:
                self.load_x(src, t + 1, None)
            if t % TPS == 0:
                S.op("dve", lambda e: e.memset(self.halo[:], 0.0), writes=self.r_halo)
            for m in range(8):
                banks = (0 + 3 * (m % 2), 1 + 3 * (m % 2), 2 + 3 * (m % 2))
                for gi in range(3):
                    def mm(e, m=m, gi=gi, pb=banks[gi]):
                        c0 = gi * D + m * 128
                        for k in range(8):
                            ins = e.matmul(self.ps[pb][:], w_in[:, k, c0:c0 + 128], self.hs[:, k, :],
                                           start=(k == 0), stop=(k == 7))
                        return ins
                    S.op("pe", mm, reads=rW[0] + self.r_hs, writes=[self.r_ps[banks[gi]]])
                pbg, pcg, pxv = banks
                q = 2 + m % 2
                u, ru = self.tmp[q], self.r_tmp[q]
                c, rc = self.tmp[4 + m % 2], self.r_tmp[4 + m % 2]
                cw = lambda kk, m=m: self.col("convw", (j * 3 + kk) * 8 + m)
                S.op("act", lambda e, c=c, pcg=pcg: e.activation(out=c[:, 0:T], in_=self.ps[pcg][:], func=AF.Identity),
                     reads=[self.r_ps[pcg]], writes=[rc])
                S.op("act", lambda e, pbg=pbg: e.activation(out=self.sd[:], in_=self.ps[pbg][:], func=AF.Identity),
                     reads=[self.r_ps[pbg]], writes=[self.r_sd])
                S.op("dve", lambda e, u=u, m=m: e.tensor_copy(out=u[:, 0:2], in_=self.halo[:, m, 0:2]),
                     reads=[self.r_halo[m]], writes=[ru])
                S.op("dve", lambda e, u=u, c=c, pxv=pxv: e.tensor_tensor(out=u[:, 2:2 + T], in0=self.ps[pxv][:],
                                                                          in1=c[:, 0:T], op=ALU.mult),
                     reads=[self.r_ps[pxv], rc], writes=[ru])
                S.op("dve", lambda e, u=u, m=m: e.tensor_copy(out=self.halo[:, m, 0:2], in_=u[:, T:T + 2]),
                     reads=[ru], writes=[self.r_halo[m]])
                S.op("act", lambda e, u=u, c=c, cw=cw: e.activation(out=c[:, 0:T], in_=u[:, 2:2 + T], func=AF.Identity,
                                                                    scale=cw(2)),
                     reads=[ru, self.r_colt], writes=[rc])
                S.op("dve", lambda e, u=u, c=c, cw=cw: e.scalar_tensor_tensor(
                    out=c[:, 0:T], in0=u[:, 1:1 + T], scalar=cw(1), in1=c[:, 0:T], op0=ALU.mult, op1=ALU.add),
                    reads=[ru, rc, self.r_colt], writes=[rc])
                S.op("dve", lambda e, u=u, c=c, cw=cw: e.scalar_tensor_tensor(
                    out=c[:, 0:T], in0=u[:, 0:T], scalar=cw(0), in1=c[:, 0:T], op0=ALU.mult, op1=ALU.add),
                    reads=[ru, rc, self.r_colt], writes=[rc])
                S.op("dve", lambda e, c=c, m=m: e.tensor_tensor(out=v[:, m, :], in0=self.sd[:],
                                                                 in1=c[:, 0:T], op=ALU.mult),
                     reads=[self.r_sd, rc], writes=[r_v[m]])
            hook = None
            if t + 1 < NT:
                self.emit_norm_stats(t + 1, l, 0)
                def hook(m2, t=t):
                    if m2 < 4:
                        self.emit_norm_chunk(t + 1, l, 0, 2 * m2)
                        self.emit_norm_chunk(t + 1, l, 0, 2 * m2 + 1)
            self.emit_oproj(t, l, w_out, rW[1], v, r_v[0:8], 8, 2, (7, 0), hook)
            toks = self.store_x(dst, t)
        return toks

    def pass_pool(self, pi, l, src, dst):
        S = self.S
        reg = self.wreg_of[pi]
        w_in = self.wview(reg, 0, 8, D)
        w_g = self.W[reg][:, 8 * D:8 * D + 2048].rearrange("p (k n) -> p k n", k=8)
        w_out = self.wview(reg, 8 * D + 2048, 8, D)
        rW = [[r, self.rG[reg]] for r in self.rW[reg]]
        v, r_v = self.g, self.r_g
        pbuf = self.sq
        HO = 15
        toks = []
        self.load_x(src, 0, None)
        self.emit_norm(0, l, 0)
        for t in range(NT):
            if t + 1 < NT:
                self.load_x(src, t + 1, None)
            if t % TPS == 0:
                S.op("dve", lambda e: e.memset(self.halo[:], 0.0), writes=self.r_halo)
            def emit_mmg(gi):
                for n2 in range(2):
                    pb = 2 + n2

                    def mmg(e, gi=gi, n2=n2, pb=pb):
                        for kk in range(2):
                            ins = e.matmul(self.ps[pb][:], w_g[:, gi * 2 + kk, n2 * 128:(n2 + 1) * 128], pbuf[kk][:],
                                           start=(kk == 0), stop=(kk == 1))
                        return ins
                    S.op("pe", mmg, reads=rW[1] + self.r_sq, writes=[self.r_ps[pb]])
                    S.op("act", lambda e, gi=gi, n2=n2, pb=pb: e.activation(
                        out=v[:, gi * 2 + n2, :], in_=self.ps[pb][:], func=AF.Identity,
                        scale=self.col("pscale", gi * 2 + n2)), reads=[self.r_ps[pb], self.r_colt],
                        writes=[r_v[gi * 2 + n2]])

            for gi in range(4):
                wnd = 2 << gi
                bufsets = []
                for kk in range(2):
                    m = gi * 2 + kk
                    pb = m % 2

                    def mm(e, m=m, pb=pb):
                        for k in range(8):
                            ins = e.matmul(self.ps[pb][:], w_in[:, k, m * 128:(m + 1) * 128], self.hs[:, k, :],
                                           start=(k == 0), stop=(k == 7))
                        return ins
                    S.op("pe", mm, reads=rW[0] + self.r_hs, writes=[self.r_ps[pb]])
                    u, ru = self.tmp[2 + kk], self.r_tmp[2 + kk]
                    if kk == 0:
                        bufs = [(self.tmp[4], self.r_tmp[4]), (self.tmp[5], self.r_tmp[5])]
                    else:
                        bufs = [(self.tmp[0], self.r_tmp[0]), (self.tmp[1], self.r_tmp[1])]
                    bufsets.append((m, pb, u, ru, bufs))
                if gi > 0:
                    emit_mmg(gi - 1)
                for (m, pb, u, ru, bufs) in bufsets:
                    S.op("dve", lambda e, u=u, m=m: e.tensor_copy(out=u[:, 0:HO], in_=self.halo[:, m, 0:HO]),
                         reads=[self.r_halo[m]], writes=[ru])
                for (m, pb, u, ru, bufs) in bufsets:
                    S.op("act", lambda e, u=u, pb=pb: e.activation(out=u[:, HO:HO + T], in_=self.ps[pb][:], func=AF.Identity),
                         reads=[self.r_ps[pb]], writes=[ru])
                for (m, pb, u, ru, bufs) in bufsets:
                    S.op("dve", lambda e, u=u, m=m: e.tensor_copy(out=self.halo[:, m, 0:HO], in_=u[:, T:T + HO]),
                         reads=[ru], writes=[self.r_halo[m]])
                state = []
                for (m, pb, u, ru, bufs) in bufsets:
                    state.append(dict(cur=u, rcur=ru, bi=0, lo=0))
                sh = 1
                while sh < wnd:
                    for kk, (m, pb, u, ru, bufs) in enumerate(bufsets):
                        stt = state[kk]
                        nb, rnb = bufs[stt["bi"]]
                        stt["bi"] ^= 1
                        lo2 = stt["lo"] + sh
                        cur, rcur = stt["cur"], stt["rcur"]
                        S.op("dve" if kk == 0 else "pool", lambda e, cur=cur, nb=nb, lo2=lo2, sh=sh: e.tensor_tensor(
                            out=nb[:, lo2:HO + T], in0=cur[:, lo2:HO + T], in1=cur[:, lo2 - sh:HO + T - sh], op=ALU.add),
                            reads=[rcur], writes=[rnb])
                        stt["cur"], stt["rcur"], stt["lo"] = nb, rnb, lo2
                    sh *= 2
                for kk, (m, pb, u, ru, bufs) in enumerate(bufsets):
                    stt = state[kk]
                    cur, rcur = stt["cur"], stt["rcur"]
                    S.op("dve", lambda e, cur=cur, u=u, kk=kk, wnd=wnd: e.scalar_tensor_tensor(
                        out=pbuf[kk][:], in0=cur[:, HO:HO + T], scalar=1.0 / wnd, in1=u[:, HO:HO + T],
                        op0=ALU.mult, op1=ALU.subtract), reads=[rcur, ru], writes=[self.r_sq[kk]])
                    if t % TPS == 0:
                        nb, rnb = bufs[stt["bi"]]
                        ic = self.colt[:, CL["invc"] + gi * 16:CL["invc"] + gi * 16 + 16]
                        S.op("dve", lambda e, cur=cur, nb=nb, ic=ic: e.tensor_tensor(
                            out=nb[:, 0:16], in0=cur[:, HO:HO + 16], in1=ic, op=ALU.mult),
                            reads=[rcur, self.r_colt], writes=[rnb])
                        S.op("dve", lambda e, nb=nb, u=u, kk=kk: e.tensor_tensor(
                            out=pbuf[kk][:, 0:16], in0=nb[:, 0:16], in1=u[:, HO:HO + 16], op=ALU.subtract),
                            reads=[rnb, ru], writes=[self.r_sq[kk]])
            emit_mmg(3)
            hook = None
            if t + 1 < NT:
                self.emit_norm_stats(t + 1, l, 0)
                def hook(m2, t=t):
                    if m2 < 4:
                        self.emit_norm_chunk(t + 1, l, 0, 2 * m2)
                        self.emit_norm_chunk(t + 1, l, 0, 2 * m2 + 1)
            self.emit_oproj(t, l, w_out, rW[2], v, r_v[0:8], 8, 2, (4, 5), hook)
            toks = self.store_x(dst, t)
        return toks

    def pass_qkv(self, pi, l, src):
        S = self.S
        reg = self.wreg_of[pi]
        oth = reg ^ 1
        w = self.wview(reg, 0, 8, 3 * D)
        rW = [[r, self.rG[reg]] for r in self.rW[reg]]
        qst, r_qst = self.g, self.r_g
        kst = self.W[oth][:, 8192:12288].rearrange("p (k n) -> p k n", k=8)
        vst = self.W[oth][:, 12288:16384].rearrange("p (k n) -> p k n", k=4)
        r_kst = [Res("kst%d" % i) for i in range(8)]
        r_vst = [Res("vst%d" % i) for i in range(4)]
        d_q, d_k, d_v = S.dsem(), S.dsem(), S.dsem()
        qv = self.qscr.rearrange("(m p) n -> p m n", p=128)
        kv = self.kscr.rearrange("(m p) n -> p m n", p=128)
        vv = self.vscr.rearrange("(n p) f -> p n f", p=128)
        mod_slots = []
        for q in range(2):
            mod_slots.append((self.W[oth][:, 16384 + q * 8192:16384 + (q + 1) * 8192].rearrange("p (k n) -> p k n", k=8),
                              Res("mstg%d" % q), S.dsem(), self.rG[oth]))
        mod_parts = [(l2, j2) for l2 in self.deferred_mods for j2 in range(6)]
        self.load_x(src, 0, None)
        self.emit_norm(0, l, 0)
        n = 0
        for t in range(NT):
            if t + 1 < NT:
                self.load_x(src, t + 1, None)
            items = [(which, hd) for which in range(2) for hd in range(8)]

            def post(n_, which, hd):
                pq = n_ % 4
                pss = 4 + n_ % 2
                st_, r_st = (qst, r_qst) if which == 0 else (kst, r_kst)
                gcol = self.col("qn") if which == 0 else self.col("kn")
                sq, rsq = self.sq[n_ % 2], self.r_sq[n_ % 2]
                S.op("pe", lambda e: e.matmul(self.ps[pss][:], self.bones[:], sq[:], start=True, stop=True),
                     reads=[rsq, self.r_ones], writes=[self.r_ps[pss]])
                sdt, rsdt = self.tmp[2 + n_ % 2], self.r_tmp[2 + n_ % 2]
                if which == 0:
                    S.op("act", lambda e: e.activation(
                        out=sdt[:, 0:T], in_=self.ps[pss][:], func=AF.Ln, bias=self.lamc[:, 3:4], scale=1.0),
                        reads=[self.r_ps[pss], self.r_lam], writes=[rsdt])
                else:
                    S.op("act", lambda e: e.activation(
                        out=sdt[:, 0:T], in_=self.ps[pss][:], func=AF.Ln, bias=self.epsc[:], scale=1.0 / 64),
                        reads=[self.r_ps[pss], self.r_ones], writes=[rsdt])
                S.op("act", lambda e: e.activation(out=sdt[:, 0:T], in_=sdt[:, 0:T], func=AF.Exp, scale=-0.5),
                     reads=[rsdt], writes=[rsdt])
                S.op("dve", lambda e: e.scalar_tensor_tensor(
                    out=st_[:, hd, :], in0=self.ps[pq][:], scalar=gcol, in1=sdt[:, 0:T], op0=ALU.mult, op1=ALU.mult),
                    reads=[self.r_ps[pq], rsdt, self.r_colt], writes=[r_st[hd]])

            prev = None
            for (which, hd) in items:
                pq = n % 4

                def mm(e, which=which, hd=hd, pq=pq):
                    c0 = which * D + hd * 128
                    for k in range(8):
                        ins = e.matmul(self.ps[pq][:], w[:, k, c0:c0 + 128], self.hs[:, k, :],
                                       start=(k == 0), stop=(k == 7))
                    return ins
                S.op("pe", mm, reads=rW[0] + self.r_hs, writes=[self.r_ps[pq]])
                sq, rsq = self.sq[n % 2], self.r_sq[n % 2]
                S.op("act", lambda e, pq=pq, sq=sq: e.activation(out=sq[:], in_=self.ps[pq][:], func=AF.Square),
                     reads=[self.r_ps[pq]], writes=[rsq])
                if prev is not None:
                    post(*prev)
                prev = (n, which, hd)
                n += 1
            post(*prev)
            for jb in range(4):
                for nh in range(2):
                    pv = (jb * 2 + nh) % 4

                    def mmv(e, jb=jb, nh=nh, pv=pv):
                        for k in range(8):
                            ins = e.matmul(self.ps[pv][:], self.hs[:, k, jb * 128:(jb + 1) * 128],
                                           w[:, k, 2 * D + nh * 512:2 * D + (nh + 1) * 512], start=(k == 0), stop=(k == 7))
                        return ins
                    S.op("pe", mmv, reads=rW[0] + self.r_hs, writes=[self.r_ps[pv]])
                    S.op("dve", lambda e, jb=jb, nh=nh, pv=pv: e.tensor_copy(
                        out=vst[:, jb, nh * 512:(nh + 1) * 512], in_=self.ps[pv][:]),
                        reads=[self.r_ps[pv]], writes=[r_vst[jb]])
            S.dma("sp", lambda e, t=t: e.dma_start(out=qv[:, :, self.xcols(t)], in_=qst[:, 0:8, :]), d_q,
                  reads=r_qst[0:8], writes=[self.dres("q", t)])
            S.dma("sp", lambda e, t=t: e.dma_start(out=kv[:, :, self.xcols(t)], in_=kst), d_k,
                  reads=r_kst, writes=[self.dres("k", t)])
            S.dma("sp", lambda e, t=t: e.dma_start(out=vv[:, t * 4:(t + 1) * 4, :], in_=vst), d_v,
                  reads=r_vst, writes=[self.dres("v", t)])
            if t < len(mod_parts):
                self.emit_mod_part(mod_parts[t][0], mod_parts[t][1], mod_slots[t % 2])
            if t + 1 < NT:
                self.emit_norm(t + 1, l, 0, ring=(0, 1, 4, 5))
        assert len(mod_parts) <= NT
        self.emit_mod_finish(self.deferred_mods)
        return []

    def pass_att(self, pi, l):
        S = self.S
        reg = self.wreg_of[pi - 1]
        oth = reg ^ 1
        lam_init = 0.8 - 0.6 * math.exp(-0.3 * l)
        WR = self.W[reg]
        sets = []
        for i in range(2):
            base = i * 12288
            sets.append(dict(
                K=WR[:, base:base + 4096], Q=WR[:, base + 4096:base + 8192],
                V=WR[:, base + 8192:base + 12288].rearrange("p (n f) -> p n f", n=32),
                rK=Res("K%d" % i), rQ=Res("Q%d" % i), rV=Res("V%d" % i),
                dK=S.dsem(), dQ=S.dsem(), dV=S.dsem()))
        for i in range(2):
            for nm in ("rK", "rQ", "rV"):
                pass
        NE = 4
        et = [WR[:, 24576 + i * 1024:24576 + (i + 1) * 1024].rearrange("p (m n) -> p m n", m=2) for i in range(NE)]
        r_et = [Res("e%d" % i) for i in range(NE)]
        masks = self.W[oth][:, 16384:32768]
        r_mask = Res("mask")
        d_mask = S.dsem()
        d_o = S.dsem()
        ov = self.oscr
        vv = self.vscr.rearrange("(n p) f -> p n f", p=128)
        all_prev = self.rW[reg] + self.rW[oth]
        S.dma("sp", lambda e: e.dma_start(out=masks, in_=self.amask), d_mask, reads=[], writes=[r_mask, self.rG[oth]])
        pl = self.ps[0]
        S.op("dve", lambda e: e.tensor_tensor(out=self.tmp[0][:, 0:1], in0=self.col("lam", 0), in1=self.col("lam", 1), op=ALU.mult),
             reads=[self.r_colt], writes=[self.r_tmp[0]])
        S.op("dve", lambda e: e.tensor_tensor(out=self.tmp[0][:, 1:2], in0=self.col("lam", 2), in1=self.col("lam", 3), op=ALU.mult),
             reads=[self.r_colt], writes=[self.r_tmp[0]])
        S.op("pe", lambda e: e.matmul(pl[:, 0:2], self.onesf[:], self.tmp[0][:, 0:2], start=True, stop=True),
             reads=[self.r_tmp[0], self.r_ones], writes=[self.r_ps[0]])
        S.op("act", lambda e: e.activation(out=self.tmp[1][:, 0:2], in_=pl[:, 0:2], func=AF.Exp),
             reads=[self.r_ps[0]], writes=[self.r_tmp[1]])
        S.op("dve", lambda e: e.scalar_tensor_tensor(out=self.lamc[:, 0:1], in0=self.tmp[1][:, 0:1], scalar=lam_init,
                                                      in1=self.tmp[1][:, 1:2], op0=ALU.add, op1=ALU.subtract),
             reads=[self.r_tmp[1]], writes=[self.r_lam])
        S.op("dve", lambda e: e.tensor_scalar(out=self.lamc[:, 1:2], in0=self.lamc[:, 0:1], scalar1=-1.0, scalar2=None, op0=ALU.mult),
             reads=[self.r_lam], writes=[self.r_lam])
        S.op("dve", lambda e: e.tensor_scalar(out=self.lamc[:, 2:3], in0=self.col("subln"), scalar1=1.0 - lam_init, scalar2=None,
                                              op0=ALU.mult), reads=[self.r_colt], writes=[self.r_lam])

        heads = [(s_, hd) for s_ in range(NSEQ) for hd in range(8)]

        def load_head(i):
            s_, hd = heads[i]
            st_ = sets[i % 2]
            cs = slice(s_ * SEQ, (s_ + 1) * SEQ)
            rd_k = [self.dres("k", t) for t in range(s_ * TPS, (s_ + 1) * TPS)]
            rd_q = [self.dres("q", t) for t in range(s_ * TPS, (s_ + 1) * TPS)]
            rd_v = [self.dres("v", t) for t in range(s_ * TPS, (s_ + 1) * TPS)]
            extra = [self.rG[reg]] if i < 2 else []
            S.dma("sp", lambda e: e.dma_start(out=st_["K"], in_=self.kscr[hd * 128:(hd + 1) * 128, cs]), st_["dK"],
                  reads=rd_k, writes=[st_["rK"]] + extra)
            S.dma("sp", lambda e: e.dma_start(out=st_["Q"], in_=self.qscr[hd * 128:(hd + 1) * 128, cs]), st_["dQ"],
                  reads=rd_q, writes=[st_["rQ"]])
            for n0 in range(0, 32, 8):
                S.dma("sp", lambda e, n0=n0: e.dma_start(out=st_["V"][:, n0:n0 + 8, :],
                                                         in_=vv[:, s_ * 32 + n0:s_ * 32 + n0 + 8, hd * 128:(hd + 1) * 128]),
                      st_["dV"], reads=rd_v, writes=[st_["rV"]])

        steps = []
        for i, (s_, hd) in enumerate(heads):
            for qt in range(TPS):
                nkb = 4 * qt + 4
                for kb in range(nkb):
                    steps.append((i, s_, hd, qt, kb, nkb))
        ecnt = [0]

        def emit_score(step, slot):
            i, s_, hd, qt, kb, nkb = step
            st_ = sets[i % 2]
            j = kb - 4 * qt
            c0 = 128 * j if j > 0 else 0
            d0 = 512 * qt - 128 * kb
            sl = slot % 2
            for m in range(2):
                pb = sl * 2 + m
                S.op("pe", lambda e, m=m, pb=pb: e.matmul(
                    self.ps[pb][:, c0:T], st_["K"][m * 64:(m + 1) * 64, kb * 128:(kb + 1) * 128],
                    st_["Q"][m * 64:(m + 1) * 64, qt * T + c0:(qt + 1) * T], start=True, stop=True),
                    reads=[st_["rK"], st_["rQ"], self.rG[reg]], writes=[self.r_ps[pb]])
            ei = slot % NE
            ebuf, reb = et[ei], r_et[ei]
            for (a0, a1, cq) in att_centers(hd):
                lo = max(a0, c0)
                if lo >= a1:
                    continue
                bc = self.col("abias", ABI[(hd, d0 + cq)])
                S.op("act", lambda e, sl=sl, ebuf=ebuf, lo=lo, a1=a1, bc=bc: e.activation(
                    out=ebuf[:, :, lo:a1], in_=self.psall[:, 2 * sl:2 * sl + 2, lo:a1], func=AF.Exp, bias=bc, scale=1.0),
                    reads=[self.r_ps[2 * sl], self.r_ps[2 * sl + 1], self.r_colt], writes=[reb])
            if j >= 0:
                mk = masks[:, (hd * 4 + j) * 512:(hd * 4 + j + 1) * 512]
                for m in range(2):
                    S.op("dve", lambda e, ebuf=ebuf, mk=mk, m=m: e.tensor_tensor(
                        out=ebuf[:, m, c0:T], in0=ebuf[:, m, c0:T], in1=mk[:, c0:T], op=ALU.mult),
                        reads=[reb, r_mask], writes=[reb])

        def emit_pv(step, slot):
            i, s_, hd, qt, kb, nkb = step
            st_ = sets[i % 2]
            j = kb - 4 * qt
            c0 = 128 * j if j > 0 else 0
            ei = slot % NE
            ebuf, reb = et[ei], r_et[ei]
            for m in range(2):
                S.op("pe", lambda e, m=m, ebuf=ebuf: e.matmul(
                    self.ps[4 + m][:, c0:T], st_["V"][:, kb, :], ebuf[:, m, c0:T], start=(kb == 0), stop=(kb == nkb - 1)),
                    reads=[st_["rV"], reb, self.rG[reg], self.rG[oth]], writes=[self.r_ps[4 + m]])
                S.op("pe", lambda e, m=m, ebuf=ebuf: e.matmul(
                    self.ps[6 + m][:, c0:T], self.ones[:], ebuf[:, m, c0:T], start=(kb == 0), stop=(kb == nkb - 1)),
                    reads=[self.r_ones, reb], writes=[self.r_ps[6 + m]])

        def emit_final(step):
            i, s_, hd, qt, kb, nkb = step
            rt = self.r_tmp
            tA = self.tmpall
            S.op("dve", lambda e: e.tensor_copy(out=tA[:, 0:2, 0:T], in_=self.psall[:, 4:6, :]),
                 reads=[self.r_ps[4], self.r_ps[5]], writes=[rt[0], rt[1]])
            S.op("act", lambda e: e.activation(out=tA[:, 2:4, 0:T], in_=self.psall[:, 6:8, :], func=AF.Identity),
                 reads=[self.r_ps[6], self.r_ps[7]], writes=[rt[2], rt[3]])
            for m in range(2):
                S.op("dve", lambda e, m=m: e.reciprocal(out=tA[:, 2 + m, 0:T], in_=tA[:, 2 + m, 0:T]),
                     reads=[rt[2 + m]], writes=[rt[2 + m]])
            for m in range(2):
                S.op("dve", lambda e, m=m: e.tensor_tensor(out=tA[:, m, 0:T], in0=tA[:, m, 0:T], in1=tA[:, 2 + m, 0:T], op=ALU.mult),
                     reads=[rt[m], rt[2 + m]], writes=[rt[m]])
            q = ecnt[0] % 2
            ecnt[0] += 1
            C, rC = self.tmp[4 + q], rt[4 + q]
            S.op("dve", lambda e: e.scalar_tensor_tensor(out=C[:, 0:T], in0=tA[:, 1, 0:T], scalar=self.lamc[:, 1:2], in1=tA[:, 0, 0:T],
                                                          op0=ALU.mult, op1=ALU.add), reads=[rt[0], rt[1], self.r_lam], writes=[rC])
            sq, rsq = self.sq[q], self.r_sq[q]

            def part2():
                S.op("act", lambda e: e.activation(out=sq[:], in_=C[:, 0:T], func=AF.Square), reads=[rC], writes=[rsq])
                S.op("pe", lambda e: e.matmul(self.ps[0][:], self.ones[:], sq[:], start=True, stop=True),
                     reads=[rsq, self.r_ones], writes=[self.r_ps[0]])
                S.op("act", lambda e: e.activation(out=self.rstd[:], in_=self.ps[0][:], func=AF.Ln, bias=self.epsc[:], scale=1.0 / 128),
                     reads=[self.r_ps[0], self.r_ones], writes=[self.r_rstd])
                S.op("act", lambda e: e.activation(out=self.rstd[:], in_=self.rstd[:], func=AF.Exp, scale=-0.5),
                     reads=[self.r_rstd], writes=[self.r_rstd])
                S.op("dve", lambda e: e.scalar_tensor_tensor(out=sq[:], in0=C[:, 0:T], scalar=self.lamc[:, 2:3], in1=self.rstd[:],
                                                              op0=ALU.mult, op1=ALU.mult), reads=[rC, self.r_rstd, self.r_lam], writes=[rsq])
                tg = s_ * TPS + qt
                S.dma("sp", lambda e: e.dma_start(out=ov[hd * 128:(hd + 1) * 128, tg * T:(tg + 1) * T], in_=sq[:]), d_o,
                      reads=[rsq], writes=[self.dres("o%d" % hd, tg)])
            return part2

        load_head(0)
        LAG = 2
        DEFER = 8
        pending = []
        for si in range(len(steps) + LAG):
            if si < len(steps):
                emit_score(steps[si], si)
            pj = si - LAG
            if pj >= 0:
                i, s_, hd, qt, kb, nkb = steps[pj]
                if qt == 0 and kb == 0 and i + 1 < len(heads):
                    load_head(i + 1)
                emit_pv(steps[pj], pj)
                if steps[pj][4] == steps[pj][5] - 1:
                    pending.append((pj + DEFER, emit_final(steps[pj])))
                while pending and pending[0][0] <= pj:
                    pending.pop(0)[1]()
        for _, fn in pending:
            fn()
        return []

    def pass_oproj(self, pi, l, src, dst):
        S = self.S
        reg = self.wreg_of[pi]
        w_out = self.wview(reg, 0, 8, D)
        rW = [[r, self.rG[reg]] for r in self.rW[reg]]
        ov = self.oscr.rearrange("(m p) n -> p m n", p=128)
        obuf = [(self.hs, self.r_hs, self.d_hs), (self.g, self.r_g[0:8], S.dsem())]
        toks = []

        def load_o(t):
            ob, rob, dob = obuf[t % 2]
            S.dma("sp", lambda e, t=t: e.dma_start(out=ob[:, 0:8, :], in_=ov[:, :, self.xcols(t)]), dob,
                  reads=[self.dres("o%d" % hd, t) for hd in range(8)], writes=rob)

        self.load_x(src, 0, None)
        load_o(0)
        for t in range(NT):
            if t + 1 < NT:
                self.load_x(src, t + 1, None)
                load_o(t + 1)
            ob, rob, dob = obuf[t % 2]
            self.emit_oproj(t, l, w_out, rW[0], ob, rob, 8, 2, (4, 5))
            toks = self.store_x(dst, t)
        return toks


def build_cols(inp):
    cols = np.zeros((128, CL.n), np.float32)

    def put(name, arr, idx=0):
        a = np.asarray(arr, np.float32)
        cols[:, CL[name] + idx:CL[name] + idx + a.shape[1]] = a
    for l in range(DEPTH):
        put("adab", vec_cols(inp["ada_b"][l]), l * 48)
        put("nmix", vec_cols(inp["norm_mix"][l]), l * 8)
        put("nffn", vec_cols(inp["norm_ffn"][l]), l * 8)
    for j in range(2):
        for k in range(3):
            put("convw", vec_cols(inp["conv_w"][j, k]), (j * 3 + k) * 8)
    put("pscale", vec_cols(inp["pool_scale"][0]))
    put("qn", np.concatenate([inp["diff_q_norm"][0]] * 2)[:, None])
    put("kn", np.concatenate([inp["diff_k_norm"][0]] * 2)[:, None])
    put("subln", np.asarray(inp["diff_subln"][0])[:, None])
    lam = np.zeros((128, 4), np.float32)
    for i, nm in enumerate(("diff_lq1", "diff_lk1", "diff_lq2", "diff_lk2")):
        lam[:64, i] = inp[nm][0]
    put("lam", lam)
    invc = np.zeros((128, 64), np.float32)
    for gi in range(4):
        w = 2 << gi
        for tt in range(16):
            invc[:, gi * 16 + tt] = 1.0 / min(tt + 1, w)
    put("invc", invc)
    ab = np.zeros((128, NAB), np.float32)
    kl = np.arange(128, dtype=np.float64)
    for (h, dlt), ci in ABI.items():
        ab[:, ci] = (2.0 ** (-(h + 1))) * (kl - dlt)
    put("abias", ab)
    return cols


def build_amask():
    m = np.zeros((128, 32, 512), np.float64)
    kl = np.arange(128)[:, None]
    ql = np.arange(512)[None, :]
    for h in range(8):
        slope = 2.0 ** (-(h + 1))
        for j in range(4):
            kp = 128 * j + kl
            allowed = (kp // 64) <= (ql // 64)
            fix = np.where(kp > ql, np.exp(-2.0 * slope * (kp - ql)), 1.0)
            m[:, h * 4 + j, :] = np.where(allowed, fix, 0.0)
    return m.reshape(128, 32 * 512).astype(np.float32).astype(mybir.dt.np(BF16))


_CACHE = {}


def get_nc(layers=(0, 1, 2, 3), stop_after=None):
    key = (tuple(layers), stop_after)
    if key not in _CACHE:
        _CACHE[key] = Builder(layers, stop_after).build()
    return _CACHE[key]


def make_in_maps(inp, ncores=NCORES):
    cols = build_cols(inp)
    amask = build_amask()
    shared = {
        "cols": cols, "amask": amask,
        "ada_w": np.ascontiguousarray(inp["ada_w"], np.float32),
        "ffn_w1": np.ascontiguousarray(inp["ffn_w1"], np.float32),
        "ffn_w3": np.ascontiguousarray(inp["ffn_w3"], np.float32),
        "ffn_w2": np.ascontiguousarray(inp["ffn_w2"], np.float32),
        "conv_w_in": np.ascontiguousarray(inp["conv_w_in"], np.float32),
        "conv_w_out": np.ascontiguousarray(inp["conv_w_out"], np.float32),
        "diff_w_qkv": np.ascontiguousarray(inp["diff_w_qkv"], np.float32),
        "diff_w_out": np.ascontiguousarray(inp["diff_w_out"], np.float32),
        "pool_w_in": np.ascontiguousarray(inp["pool_w_in"], np.float32),
        "pool_w_group": np.ascontiguousarray(inp["pool_w_group"], np.float32),
        "pool_w_out": np.ascontiguousarray(inp["pool_w_out"], np.float32),
    }
    x = np.asarray(inp["x"], np.float32)
    c = np.asarray(inp["c"], np.float32)
    maps = []
    for ci in range(ncores):
        xc = x[ci * NSEQ:(ci + 1) * NSEQ].reshape(NTOK, D)
        m = dict(shared)
        m["xin"] = np.ascontiguousarray(xc.T)
        cc = c[ci * NSEQ:(ci + 1) * NSEQ]
        m["ccol"] = np.ascontiguousarray(cc.reshape(NSEQ, 8, 128).transpose(2, 1, 0).reshape(128, 16))
        maps.append(m)
    return maps


def kernel(**inputs):
    nc = get_nc()
    maps = make_in_maps(inputs)
    res = run_bass_kernel_spmd(nc, maps, core_ids=list(range(NCORES)))
    out = np.empty((NCORES * NSEQ, SEQ, D), np.float32)
    for ci in range(NCORES):
        o = res.results[ci]["out"]
        out[ci * NSEQ:(ci + 1) * NSEQ] = o.T.reshape(NSEQ, SEQ, D)
    return out
```

```python
import math
from contextlib import ExitStack
import numpy as np
import concourse.bass as bass
import concourse.mybir as mybir
from concourse.bass_utils import run_bass_kernel_spmd

F32 = mybir.dt.float32
BF16 = mybir.dt.bfloat16
AF = mybir.ActivationFunctionType
ALU = mybir.AluOpType

ENGS = ("pe", "act", "dve", "pool", "sp")
D = 1024
SEQ = 4096
NSEQ = 2
NTOK = NSEQ * SEQ
T = 512
NT = NTOK // T
TPS = SEQ // T
DFF = 2816
HF = DFF // 2
NFC = HF // 128
DEPTH = 4
EPS = 1e-6
WREG = 33792
NCORES = 8


class Res:
    __slots__ = ("name", "last_w", "readers")

    def __init__(self, name=""):
        self.name = name
        self.last_w = None
        self.readers = {}


class DSem:
    __slots__ = ("key", "count")

    def __init__(self, key):
        self.key = key
        self.count = 0


class Sched:
    def __init__(self, nc):
        self.nc = nc
        self.ops = {e: [] for e in ENGS}
        self.cnt = {e: 0 for e in ENGS}
        self.seen = {e: {} for e in ENGS}
        self.dsems = []
        self.dsem_by_key = {}

    def dsem(self):
        d = DSem("d%d" % len(self.dsems))
        self.dsems.append(d)
        self.dsem_by_key[d.key] = d
        return d

    def _deps(self, eng, reads, writes):
        deps = {}
        skip = "pe" if eng == "pe" else None
        for r in reads:
            t = r.last_w
            if t is not None and t[0] != skip and deps.get(t[0], 0) < t[1]:
                deps[t[0]] = t[1]
        for w in writes:
            t = w.last_w
            if t is not None and t[0] != skip and deps.get(t[0], 0) < t[1]:
                deps[t[0]] = t[1]
            for k, v in w.readers.items():
                if k != skip and deps.get(k, 0) < v:
                    deps[k] = v
        seen = self.seen[eng]
        for k, v in deps.items():
            d = self.dsem_by_key.get(k)
            if d is not None and d.count > v:
                v = d.count
            if seen.get(k, 0) < v:
                seen[k] = v
                self.ops[eng].append(("wait", k, v))

    def _mark(self, tok, reads, writes):
        k, v = tok
        for r in reads:
            if r.readers.get(k, 0) < v:
                r.readers[k] = v
        for w in writes:
            w.last_w = tok
            w.readers = {}

    def op(self, eng, fn, reads=(), writes=()):
        self._deps(eng, reads, writes)
        self.cnt[eng] += 1
        tok = (eng, self.cnt[eng])
        self.ops[eng].append(("ins", fn))
        self._mark(tok, reads, writes)
        return tok

    def dma(self, eng, fn, ds, reads=(), writes=()):
        self._deps(eng, reads, writes)
        ds.count += 16
        tok = (ds.key, ds.count)
        self.ops[eng].append(("dma", fn, ds.key))
        self._mark(tok, reads, writes)
        return tok

    def wait_tok(self, eng, tok):
        k, v = tok
        if self.seen[eng].get(k, 0) < v:
            self.seen[eng][k] = v
            self.ops[eng].append(("wait", k, v))

    def emit(self):
        nc = self.nc
        with ExitStack() as st:
            sems = {}
            for e in ENGS:
                sems[e] = st.enter_context(nc.semaphore("s_" + e))
            for d in self.dsems:
                sems[d.key] = st.enter_context(nc.semaphore("s_" + d.key))
            block = st.enter_context(nc.Block())

            def run(ename):
                def body(eng):
                    for item in self.ops[ename]:
                        if item[0] == "wait":
                            eng.wait_ge(sems[item[1]], item[2])
                        elif item[0] == "ins":
                            item[1](eng).then_inc(sems[ename], 1)
                        else:
                            item[1](eng).then_inc(sems[item[2]], 16)
                return body

            block.tensor(run("pe"))
            block.scalar(run("act"))
            block.vector(run("dve"))
            block.gpsimd(run("pool"))
            block.sync(run("sp"))


class Cols:
    def __init__(self):
        self.off = {}
        self.n = 0

    def add(self, name, n):
        self.off[name] = self.n
        self.n += n

    def __getitem__(self, name):
        return self.off[name]


def col_layout():
    c = Cols()
    c.add("adab", DEPTH * 48)
    c.add("nmix", DEPTH * 8)
    c.add("nffn", DEPTH * 8)
    c.add("convw", 2 * 3 * 8)
    c.add("pscale", 8)
    c.add("qn", 1)
    c.add("kn", 1)
    c.add("subln", 1)
    c.add("lam", 4)
    c.add("invc", 64)
    c.add("abias", NAB)
    return c


def att_centers(h):
    if h == 0:
        return [(i * 256, (i + 1) * 256, i * 256 + 128) for i in range(2)]
    return [(0, 512, 256)]


def att_bias_index():
    idx = {}
    n = 0
    for h in range(8):
        for qt in range(TPS):
            for kb in range(4 * qt + 4):
                d0 = 512 * qt - 128 * kb
                for (_, _, cq) in att_centers(h):
                    key = (h, d0 + cq)
                    if key not in idx:
                        idx[key] = n
                        n += 1
    return idx, n


ABI, NAB = att_bias_index()
CL = col_layout()


def vec_cols(v):
    v = np.asarray(v, np.float32)
    return np.ascontiguousarray(v.reshape(-1, 128).T)


class Builder:
    def __init__(self, layers=(0, 1, 2, 3), stop_after=None):
        self.layers = layers
        self.stop_after = stop_after
        self.nc = nc = bass.Bass("TRN2", target_bir_lowering=False)
        self.S = Sched(nc)
        dt = nc.dram_tensor
        self.xin = dt("xin", [D, NTOK], F32, kind="ExternalInput").ap()
        self.out = dt("out", [D, NTOK], F32, kind="ExternalOutput").ap()
        self.ccol = dt("ccol", [128, 16], F32, kind="ExternalInput").ap()
        self.cols = dt("cols", [128, CL.n], F32, kind="ExternalInput").ap()
        self.ada_w = dt("ada_w", [DEPTH, D, 6 * D], F32, kind="ExternalInput").ap()
        self.ffn_w1 = dt("ffn_w1", [DEPTH, D, DFF], F32, kind="ExternalInput").ap()
        self.ffn_w3 = dt("ffn_w3", [DEPTH, D, DFF], F32, kind="ExternalInput").ap()
        self.ffn_w2 = dt("ffn_w2", [DEPTH, DFF, D], F32, kind="ExternalInput").ap()
        self.conv_w_in = dt("conv_w_in", [2, D, 3 * D], F32, kind="ExternalInput").ap()
        self.conv_w_out = dt("conv_w_out", [2, D, D], F32, kind="ExternalInput").ap()
        self.diff_w_qkv = dt("diff_w_qkv", [1, D, 3 * D], F32, kind="ExternalInput").ap()
        self.diff_w_out = dt("diff_w_out", [1, D, D], F32, kind="ExternalInput").ap()
        self.pool_w_in = dt("pool_w_in", [1, D, D], F32, kind="ExternalInput").ap()
        self.pool_w_group = dt("pool_w_group", [1, 4, 256, 256], F32, kind="ExternalInput").ap()
        self.pool_w_out = dt("pool_w_out", [1, D, D], F32, kind="ExternalInput").ap()
        self.amask = dt("amask", [128, 32 * 512], BF16, kind="ExternalInput").ap()
        self.xa = dt("xa", [D, NTOK], F32).ap()
        self.xb = dt("xb", [D, NTOK], F32).ap()
        self.hscr = dt("hscr", [D, NTOK], BF16).ap()
        self.qscr = dt("qscr", [D, NTOK], BF16).ap()
        self.kscr = dt("kscr", [D, NTOK], BF16).ap()
        self.vscr = dt("vscr", [NTOK, D], BF16).ap()
        self.oscr = dt("oscr", [D, NTOK], BF16).ap()
        self.r_dram = {}

    def dres(self, name, t):
        key = (name, t)
        if key not in self.r_dram:
            self.r_dram[key] = Res("%s%d" % key)
        return self.r_dram[key]

    def build(self):
        nc, S = self.nc, self.S
        with ExitStack() as st:
            sb = lambda name, shape, dtp: st.enter_context(nc.sbuf_tensor(name, shape, dtp))
            self.W = [sb("WA", [128, WREG], BF16), sb("WB", [128, WREG], BF16)]
            self.rW = [[Res("W%d_%d" % (i, j)) for j in range(4)] for i in range(2)]
            self.dW = [[S.dsem() for j in range(4)] for i in range(2)]
            self.rG = [Res("G0"), Res("G1")]
            self.xs = [sb("xs%d" % i, [128, 8, T], F32) for i in range(2)]
            self.r_xs = [[Res("xs%d_%d" % (i, m)) for m in range(8)] for i in range(2)]
            self.d_xs = [S.dsem() for i in range(2)]
            self.d_st = [S.dsem() for i in range(2)]
            self.hs = sb("hs", [128, 8, T], BF16)
            self.r_hs = [Res("hs%d" % m) for m in range(8)]
            self.d_hs = S.dsem()
            self.d_hst = S.dsem()
            self.g = sb("g", [128, NFC, T], BF16)
            self.r_g = [Res("g%d" % m) for m in range(NFC)]
            self.sq = [sb("sq%d" % i, [128, T], BF16) for i in range(2)]
            self.r_sq = [Res("sq%d" % i) for i in range(2)]
            self.sd = sb("sd", [128, T], F32)
            self.r_sd = Res("sd")
            self.rstd = sb("rstd", [128, T], F32)
            self.r_rstd = Res("rstd")
            self.tmpall = sb("tmpall", [128, 6, 528], F32)
            self.tmp = [self.tmpall[:, i, :] for i in range(6)]
            self.r_tmp = [Res("tmp%d" % i) for i in range(6)]
            self.halo = sb("halo", [128, 8, 16], F32)
            self.r_halo = [Res("halo%d" % m) for m in range(8)]
            self.colt = sb("colt", [128, CL.n], F32)
            self.r_colt = Res("colt")
            self.modT = sb("modT", [128, DEPTH * 48, 2], F32)
            self.r_mod = Res("mod")
            self.Acol = sb("Acol", [128, DEPTH * 2 * 8, 2], F32)
            self.r_A = Res("A")
            self.cond = sb("cond", [128, 16], BF16)
            self.r_cond = Res("cond")
            self.csb = sb("csb", [128, 16], F32)
            self.ones = sb("ones", [128, 128], BF16)
            self.r_ones = Res("ones")
            self.bones = sb("bones", [128, 128], BF16)
            self.epsc = sb("epsc", [128, 1], F32)
            self.lamc = sb("lamc", [128, 8], F32)
            self.onesf = sb("onesf", [128, 128], F32)
            self.r_lam = Res("lam")
            self.psall = st.enter_context(nc.psum_tensor("psall", [128, 8, T], F32))
            self.ps = [self.psall[:, i, :] for i in range(8)]
            self.r_ps = [Res("ps%d" % i) for i in range(8)]
            self.d_misc = S.dsem()

            self.prologue()
            src = self.xin
            passes = []
            for l in self.layers:
                kind = l % 3
                if kind == 0:
                    passes.append(("conv", l))
                elif kind == 1:
                    passes.append(("qkv", l))
                    passes.append(("att", l))
                    passes.append(("oproj", l))
                else:
                    passes.append(("pool", l))
                if self.stop_after == ("mix", l):
                    break
                passes.append(("ffn", l, 0))
                passes.append(("ffn", l, 1))
            self.passes = passes
            self.wreg_of = {}
            reg = 0
            for i, p in enumerate(passes):
                if p[0] == "att":
                    self.wreg_of[i] = None
                    continue
                self.wreg_of[i] = reg
                reg ^= 1
            self.loaded = set()
            self.load_weights(0)
            scr = [self.xa, self.xb]
            si = 0
            last_tok = []
            for i, p in enumerate(passes):
                nxt = i + 1
                while nxt < len(passes) and passes[nxt][0] == "att":
                    nxt += 1
                if nxt < len(passes) and p[0] != "att":
                    self.load_weights(nxt)
                is_last = (i == len(passes) - 1)
                reads_x = p[0] not in ("att",)
                writes_x = p[0] not in ("att", "qkv")
                dst = None
                if writes_x:
                    dst = self.out if is_last else scr[si]
                if p[0] == "conv":
                    toks = self.pass_conv(i, p[1], src, dst)
                elif p[0] == "pool":
                    toks = self.pass_pool(i, p[1], src, dst)
                elif p[0] == "ffn":
                    toks = self.pass_ffn(i, p[1], p[2], src, dst)
                elif p[0] == "qkv":
                    toks = self.pass_qkv(i, p[1], src)
                elif p[0] == "att":
                    toks = self.pass_att(i, p[1])
                elif p[0] == "oproj":
                    toks = self.pass_oproj(i, p[1], src, dst)
                if writes_x:
                    src = dst
                    si ^= 1
                    last_tok = toks
            for ds in self.d_st:
                S.wait_tok("sp", (ds.key, ds.count))
            S.emit()
        return nc

    def wview(self, reg, off, k, n):
        return self.W[reg][:, off:off + k * n].rearrange("p (k n) -> p k n", k=k)

    def load_w(self, dst3, src2, res, ds, guard=None):
        S = self.S
        wr = [res] if guard is None else [res, guard]
        sv = src2.rearrange("(k p) n -> p k n", p=128)
        K, N = dst3.shape[1], dst3.shape[2]
        kstep = 4
        for k0 in range(0, K, kstep):
            k1 = min(K, k0 + kstep)
            for c0 in range(0, N, 1024):
                c1 = min(N, c0 + 1024)
                S.dma("pool", lambda e, k0=k0, k1=k1, c0=c0, c1=c1: e.dma_start(
                    out=dst3[:, k0:k1, c0:c1], in_=sv[:, k0:k1, c0:c1]), ds, writes=wr)
                wr = [res]

    def load_weights(self, i):
        if i in self.loaded:
            return
        self.loaded.add(i)
        p = self.passes[i]
        reg = self.wreg_of[i]
        if reg is None:
            return
        rW, dW = self.rW[reg], self.dW[reg]
        if p[0] == "ffn":
            l, hf = p[1], p[2]
            self.load_w(self.wview(reg, 0, 8, HF), self.ffn_w1[l][:, hf * HF:(hf + 1) * HF], rW[0], dW[0], self.rG[reg])
            self.load_w(self.wview(reg, 8 * HF, 8, HF), self.ffn_w3[l][:, hf * HF:(hf + 1) * HF], rW[1], dW[1])
            self.load_w(self.wview(reg, 16 * HF, NFC, D), self.ffn_w2[l][hf * HF:(hf + 1) * HF, :], rW[2], dW[2])
        elif p[0] == "conv":
            j = p[1] // 3
            self.load_w(self.wview(reg, 0, 8, 3 * D), self.conv_w_in[j], rW[0], dW[0], self.rG[reg])
            self.load_w(self.wview(reg, 24 * D, 8, D), self.conv_w_out[j], rW[1], dW[1])
        elif p[0] == "pool":
            self.load_w(self.wview(reg, 0, 8, D), self.pool_w_in[0], rW[0], dW[0], self.rG[reg])
            for gi in range(4):
                self.load_w(self.W[reg][:, 8 * D + gi * 512:8 * D + (gi + 1) * 512].rearrange("p (k n) -> p k n", k=2),
                            self.pool_w_group[0][gi], rW[1], dW[1])
            self.load_w(self.wview(reg, 8 * D + 2048, 8, D), self.pool_w_out[0], rW[2], dW[2])
        elif p[0] == "qkv":
            self.load_w(self.wview(reg, 0, 8, 3 * D), self.diff_w_qkv[0], rW[0], dW[0], self.rG[reg])
        elif p[0] == "oproj":
            self.load_w(self.wview(reg, 0, 8, D), self.diff_w_out[0], rW[0], dW[0], self.rG[reg])

    def col(self, name, idx=0):
        c = CL[name] + idx
        return self.colt[:, c:c + 1]

    def modc(self, l, j, m, b):
        return self.modT[:, l * 48 + j * 8 + m, b:b + 1]

    def Ac(self, l, which, m, b):
        return self.Acol[:, (l * 2 + which) * 8 + m, b:b + 1]

    def prologue(self):
        nc, S = self.nc, self.S
        r_c = Res("c")
        S.dma("sp", lambda e: e.dma_start(out=self.colt[:], in_=self.cols), self.d_misc, writes=[self.r_colt])
        S.dma("sp", lambda e: e.dma_start(out=self.csb[:], in_=self.ccol), S.dsem(), writes=[r_c])
        S.op("dve", lambda e: e.memset(self.ones[:], 1.0), writes=[self.r_ones])
        S.op("dve", lambda e: e.memset(self.epsc[:], EPS), writes=[self.r_ones])
        S.op("dve", lambda e: e.memset(self.onesf[:], 1.0), writes=[self.r_ones])
        S.op("dve", lambda e: e.memset(self.lamc[:], 64.0 * EPS), writes=[self.r_lam])
        S.op("dve", lambda e: e.memset(self.bones[:], 0.0), writes=[self.r_ones])
        S.op("dve", lambda e: e.memset(self.bones[0:64, 0:64], 1.0), writes=[self.r_ones])
        S.op("dve", lambda e: e.memset(self.bones[64:128, 64:128], 1.0), writes=[self.r_ones])
        S.op("act", lambda e: e.activation(out=self.cond[:], in_=self.csb[:], func=AF.Silu),
             reads=[r_c], writes=[self.r_cond])
        condv = self.cond[:].rearrange("p (k b) -> p k b", b=2)
        psm = self.ps[7][:, 0:DEPTH * 96].rearrange("p (c b) -> p c b", b=2)
        slots = []
        for reg in range(2):
            for q in range(4):
                slots.append((self.wview(reg, q * 8 * D, 8, D), self.rW[reg][q], self.dW[reg][q], self.rG[reg]))
        has_qkv = any(l % 3 == 1 for l in self.layers)
        now = [l for i, l in enumerate(self.layers) if (i < 2 or not has_qkv)]
        self.deferred_mods = [l for l in self.layers if l not in now]
        self.psm = psm
        self.condv = condv
        n = 0
        for l in now:
            for j in range(6):
                self.emit_mod_part(l, j, slots[n % len(slots)])
                n += 1
        self.emit_mod_finish(now, first=True)

    def emit_mod_part(self, l, j, slot):
        S = self.S
        stg, rs, ds, rg = slot
        psm, condv = self.psm, self.condv
        self.load_w(stg, self.ada_w[l][:, j * D:(j + 1) * D], rs, ds)

        def mm(e):
            for m in range(8):
                cidx = l * 48 + j * 8 + m
                for k in range(8):
                    ins = e.matmul(psm[:, cidx, :], stg[:, k, m * 128:(m + 1) * 128], condv[:, k, :],
                                   start=(k == 0), stop=(k == 7))
            return ins
        S.op("pe", mm, reads=[rs, rg, self.r_cond], writes=[self.r_ps[7]])

    def emit_mod_finish(self, layers, first=False):
        S = self.S
        psm = self.psm
        if not layers:
            return
        c0 = min(layers) * 48
        c1 = (max(layers) + 1) * 48
        if first:
            c0, c1 = 0, DEPTH * 48
        adab = self.colt[:, CL["adab"] + c0:CL["adab"] + c1]
        for b in range(2):
            S.op("dve", lambda e, b=b: e.tensor_tensor(out=self.modT[:, c0:c1, b], in0=psm[:, c0:c1, b], in1=adab, op=ALU.add),
                 reads=[self.r_ps[7], self.r_colt], writes=[self.r_mod])
        for l in layers:
            for which in range(2):
                gname = "nmix" if which == 0 else "nffn"
                gcols = self.colt[:, CL[gname] + l * 8:CL[gname] + l * 8 + 8]
                jsc = 1 if which == 0 else 4
                for b in range(2):
                    S.op("dve", lambda e, l=l, which=which, b=b, gcols=gcols, jsc=jsc: e.scalar_tensor_tensor(
                        out=self.Acol[:, (l * 2 + which) * 8:(l * 2 + which) * 8 + 8, b],
                        in0=self.modT[:, l * 48 + jsc * 8:l * 48 + jsc * 8 + 8, b], scalar=1.0, in1=gcols,
                        op0=ALU.add, op1=ALU.mult), reads=[self.r_mod, self.r_colt], writes=[self.r_A])

    def xcols(self, t):
        return slice(t * T, (t + 1) * T)

    def load_x(self, src, t, srcname):
        S = self.S
        slot = t % 2
        sv = src.rearrange("(m p) n -> p m n", p=128)
        reads = [] if src is self.xin else [self.dres(id(src), t)]
        for m0 in (0, 4):
            S.dma("sp", lambda e, m0=m0: e.dma_start(out=self.xs[slot][:, m0:m0 + 4, :],
                                                     in_=sv[:, m0:m0 + 4, self.xcols(t)]),
                  self.d_xs[slot], reads=reads, writes=self.r_xs[slot][m0:m0 + 4])

    def store_x(self, dst, t):
        S = self.S
        slot = t % 2
        dv = dst.rearrange("(m p) n -> p m n", p=128)
        toks = []
        for m0 in (0, 4):
            toks.append(S.dma("sp", lambda e, m0=m0: e.dma_start(out=dv[:, m0:m0 + 4, self.xcols(t)],
                                                                 in_=self.xs[slot][:, m0:m0 + 4, :]),
                              self.d_st[slot], reads=self.r_xs[slot][m0:m0 + 4], writes=[self.dres(id(dst), t)]))
        return toks

    def emit_norm(self, t, l, which, ring=(0, 1)):
        self.emit_norm_stats(t, l, which)
        for m in range(8):
            self.emit_norm_chunk(t, l, which, m, ring)

    def emit_norm_stats(self, t, l, which):
        S = self.S
        slot = t % 2
        xs, rx = self.xs[slot], self.r_xs[slot]
        pst, rpst = self.ps[6], self.r_ps[6]
        for m in range(8):
            q = m % 2
            S.op("act", lambda e, m=m, q=q: e.activation(out=self.sq[q][:], in_=xs[:, m, :], func=AF.Square),
                 reads=[rx[m]], writes=[self.r_sq[q]])
            S.op("pe", lambda e, m=m, q=q: e.matmul(pst[:], self.ones[:], self.sq[q][:], start=(m == 0), stop=(m == 7)),
                 reads=[self.r_sq[q], self.r_ones], writes=[rpst])
        S.op("act", lambda e: e.activation(out=self.rstd[:], in_=pst[:], func=AF.Ln, bias=self.epsc[:], scale=1.0 / D),
             reads=[rpst, self.r_ones], writes=[self.r_rstd])
        S.op("act", lambda e: e.activation(out=self.rstd[:], in_=self.rstd[:], func=AF.Exp, scale=-0.5),
             reads=[self.r_rstd], writes=[self.r_rstd])

    def emit_norm_chunk(self, t, l, which, m, ring=(0, 1)):
        S = self.S
        slot = t % 2
        b = t // TPS
        xs, rx = self.xs[slot], self.r_xs[slot]
        jsh = 0 if which == 0 else 3
        q = ring[m % len(ring)]
        tmp, rt = self.tmp[q], self.r_tmp[q]
        S.op("dve", lambda e: e.scalar_tensor_tensor(
            out=tmp[:, 0:T], in0=xs[:, m, :], scalar=self.Ac(l, which, m, b), in1=self.rstd[:],
            op0=ALU.mult, op1=ALU.mult), reads=[rx[m], self.r_rstd, self.r_A], writes=[rt])
        S.op("act", lambda e: e.activation(
            out=self.hs[:, m, :], in_=tmp[:, 0:T], func=AF.Identity, bias=self.modc(l, jsh, m, b), scale=1.0),
            reads=[rt, self.r_mod], writes=[self.r_hs[m]])

    def emit_oproj(self, t, l, wout, r_wout, vin, r_vin, nk, jg, psbanks, hook=None):
        S = self.S
        slot = t % 2
        b = t // TPS
        xs, rx = self.xs[slot], self.r_xs[slot]
        for m2 in range(8):
            pb = psbanks[m2 % len(psbanks)]

            def mm(e, m2=m2, pb=pb):
                for k in range(nk):
                    ins = e.matmul(self.ps[pb][:], wout[:, k, m2 * 128:(m2 + 1) * 128], vin[:, k, :],
                                   start=(k == 0), stop=(k == nk - 1))
                return ins
            S.op("pe", mm, reads=list(r_wout) + list(r_vin), writes=[self.r_ps[pb]])
            S.op("dve", lambda e, m2=m2, pb=pb: e.scalar_tensor_tensor(
                out=xs[:, m2, :], in0=self.ps[pb][:], scalar=self.modc(l, jg, m2, b), in1=xs[:, m2, :],
                op0=ALU.mult, op1=ALU.add), reads=[self.r_ps[pb], rx[m2], self.r_mod], writes=[rx[m2]])
            if hook is not None:
                hook(m2)

    def pass_ffn(self, pi, l, hf, src, dst):
        S = self.S
        reg = self.wreg_of[pi]
        w1 = self.wview(reg, 0, 8, HF)
        w3 = self.wview(reg, 8 * HF, 8, HF)
        w2 = self.wview(reg, 16 * HF, NFC, D)
        rW = [[r, self.rG[reg]] for r in self.rW[reg]]
        hv = self.hscr.rearrange("(m p) n -> p m n", p=128)
        toks = []

        def get_h(t):
            if hf == 0:
                self.emit_norm(t, l, 1)
                S.dma("sp", lambda e, t=t: e.dma_start(out=hv[:, :, self.xcols(t)], in_=self.hs[:]),
                      self.d_hst, reads=self.r_hs, writes=[self.dres("h", t)])
            else:
                S.dma("sp", lambda e, t=t: e.dma_start(out=self.hs[:], in_=hv[:, :, self.xcols(t)]),
                      self.d_hs, reads=[self.dres("h", t)], writes=self.r_hs)

        self.load_x(src, 0, None)
        get_h(0)
        for t in range(NT):
            if t + 1 < NT:
                self.load_x(src, t + 1, None)
            for f in range(NFC):
                pa, pb = f % 2, 2 + f % 2

                def mm1(e, f=f, pa=pa):
                    for k in range(8):
                        ins = e.matmul(self.ps[pa][:], w1[:, k, f * 128:(f + 1) * 128], self.hs[:, k, :],
                                       start=(k == 0), stop=(k == 7))
                    return ins

                def mm3(e, f=f, pb=pb):
                    for k in range(8):
                        ins = e.matmul(self.ps[pb][:], w3[:, k, f * 128:(f + 1) * 128], self.hs[:, k, :],
                                       start=(k == 0), stop=(k == 7))
                    return ins
                S.op("pe", mm1, reads=rW[0] + self.r_hs, writes=[self.r_ps[pa]])
                S.op("pe", mm3, reads=rW[1] + self.r_hs, writes=[self.r_ps[pb]])
                q = 2 + f % 2
                S.op("act", lambda e, pa=pa, q=q: e.activation(out=self.tmp[q][:, 0:T], in_=self.ps[pa][:], func=AF.Silu),
                     reads=[self.r_ps[pa]], writes=[self.r_tmp[q]])
                S.op("dve", lambda e, f=f, pb=pb, q=q: e.tensor_tensor(out=self.g[:, f, :], in0=self.ps[pb][:],
                                                                        in1=self.tmp[q][:, 0:T], op=ALU.mult),
                     reads=[self.r_ps[pb], self.r_tmp[q]], writes=[self.r_g[f]])
            hook = None
            if t + 1 < NT:
                if hf == 0:
                    self.emit_norm_stats(t + 1, l, 1)

                    def hook(m2, t=t):
                        if m2 < 4:
                            self.emit_norm_chunk(t + 1, l, 1, 2 * m2, (0, 1, 4, 5))
                            self.emit_norm_chunk(t + 1, l, 1, 2 * m2 + 1, (0, 1, 4, 5))
                        if m2 == 4:
                            S.dma("sp", lambda e: e.dma_start(out=hv[:, :, self.xcols(t + 1)], in_=self.hs[:]),
                                  self.d_hst, reads=self.r_hs, writes=[self.dres("h", t + 1)])
                else:
                    get_h(t + 1)
            self.emit_oproj(t, l, w2, rW[2], self.g, self.r_g, NFC, 5, (4, 5), hook)
            toks = self.store_x(dst, t)
        return toks

    def pass_conv(self, pi, l, src, dst):
        S = self.S
        reg = self.wreg_of[pi]
        j = l // 3
        w_in = self.wview(reg, 0, 8, 3 * D)
        w_out = self.wview(reg, 24 * D, 8, D)
        rW = [[r, self.rG[reg]] for r in self.rW[reg]]
        v, r_v = self.g, self.r_g
        toks = []
        self.load_x(src, 0, None)
        self.emit_norm(0, l, 0)
        for t in range(NT):
            if t + 1 < NT:
                self.load_x(src, t + 1, None)
            if t % TPS == 0:
                S.op("dve", lambda e: e.memset(self.halo[:], 0.0), writes=self.r_halo)
            for m in range(8):
                banks = (0 + 3 * (m % 2), 1 + 3 * (m % 2), 2 + 3 * (m % 2))
                for gi in range(3):
                    def mm(e, m=m, gi=gi, pb=banks[gi]):
                        c0 = gi * D + m * 128
                        for k in range(8):
                            ins = e.matmul(self.ps[pb][:], w_in[:, k, c0:c0 + 128], self.hs[:, k, :],
                                           start=(k == 0), stop=(k == 7))
                        return ins
                    S.op("pe", mm, reads=rW[0] + self.r_hs, writes=[self.r_ps[banks[gi]]])
                pbg, pcg, pxv = banks
                q = 2 + m % 2
                u, ru = self.tmp[q], self.r_tmp[q]
                c, rc = self.tmp[4 + m % 2], self.r_tmp[4 + m % 2]
                cw = lambda kk, m=m: self.col("convw", (j * 3 + kk) * 8 + m)
                S.op("act", lambda e, c=c, pcg=pcg: e.activation(out=c[:, 0:T], in_=self.ps[pcg][:], func=AF.Identity),
                     reads=[self.r_ps[pcg]], writes=[rc])
                S.op("act", lambda e, pbg=pbg: e.activation(out=self.sd[:], in_=self.ps[pbg][:], func=AF.Identity),
                     reads=[self.r_ps[pbg]], writes=[self.r_sd])
                S.op("dve", lambda e, u=u, m=m: e.tensor_copy(out=u[:, 0:2], in_=self.halo[:, m, 0:2]),
                     reads=[self.r_halo[m]], writes=[ru])
                S.op("dve", lambda e, u=u, c=c, pxv=pxv: e.tensor_tensor(out=u[:, 2:2 + T], in0=self.ps[pxv][:],
                                                                          in1=c[:, 0:T], op=ALU.mult),
                     reads=[self.r_ps[pxv], rc], writes=[ru])
                S.op("dve", lambda e, u=u, m=m: e.tensor_copy(out=self.halo[:, m, 0:2], in_=u[:, T:T + 2]),
                     reads=[ru], writes=[self.r_halo[m]])
                S.op("act", lambda e, u=u, c=c, cw=cw: e.activation(out=c[:, 0:T], in_=u[:, 2:2 + T], func=AF.Identity,
                                                                    scale=cw(2)),
                     reads=[ru, self.r_colt], writes=[rc])
                S.op("dve", lambda e, u=u, c=c, cw=cw: e.scalar_tensor_tensor(
                    out=c[:, 0:T], in0=u[:, 1:1 + T], scalar=cw(1), in1=c[:, 0:T], op0=ALU.mult, op1=ALU.add),
                    reads=[ru, rc, self.r_colt], writes=[rc])
                S.op("dve", lambda e, u=u, c=c, cw=cw: e.scalar_tensor_tensor(
                    out=c[:, 0:T], in0=u[:, 0:T], scalar=cw(0), in1=c[:, 0:T], op0=ALU.mult, op1=ALU.add),
                    reads=[ru, rc, self.r_colt], writes=[rc])
                S.op("dve", lambda e, c=c, m=m: e.tensor_tensor(out=v[:, m, :], in0=self.sd[:],
                                                                 in1=c[:, 0:T], op=ALU.mult),
                     reads=[self.r_sd, rc], writes=[r_v[m]])
            hook = None
            if t + 1 < NT:
                self.emit_norm_stats(t + 1, l, 0)
                def hook(m2, t=t):
                    if m2 < 4:
                        self.emit_norm_chunk(t + 1, l, 0, 2 * m2)
                        self.emit_norm_chunk(t + 1, l, 0, 2 * m2 + 1)
            self.emit_oproj(t, l, w_out, rW[1], v, r_v[0:8], 8, 2, (7, 0), hook)
            toks = self.store_x(dst, t)
        return toks

    def pass_pool(self, pi, l, src, dst):
        S = self.S
        reg = self.wreg_of[pi]
        w_in = self.wview(reg, 0, 8, D)
        w_g = self.W[reg][:, 8 * D:8 * D + 2048].rearrange("p (k n) -> p k n", k=8)
        w_out = self.wview(reg, 8 * D + 2048, 8, D)
        rW = [[r, self.rG[reg]] for r in self.rW[reg]]
        v, r_v = self.g, self.r_g
        pbuf = self.sq
        HO = 15
        toks = []
        self.load_x(src, 0, None)
        self.emit_norm(0, l, 0)
        for t in range(NT):
            if t + 1 < NT:
                self.load_x(src, t + 1, None)
            if t % TPS == 0:
                S.op("dve", lambda e: e.memset(self.halo[:], 0.0), writes=self.r_halo)
            def emit_mmg(gi):
                for n2 in range(2):
                    pb = 2 + n2

                    def mmg(e, gi=gi, n2=n2, pb=pb):
                        for kk in range(2):
                            ins = e.matmul(self.ps[pb][:], w_g[:, gi * 2 + kk, n2 * 128:(n2 + 1) * 128], pbuf[kk][:],
                                           start=(kk == 0), stop=(kk == 1))
                        return ins
                    S.op("pe", mmg, reads=rW[1] + self.r_sq, writes=[self.r_ps[pb]])
                    S.op("act", lambda e, gi=gi, n2=n2, pb=pb: e.activation(
                        out=v[:, gi * 2 + n2, :], in_=self.ps[pb][:], func=AF.Identity,
                        scale=self.col("pscale", gi * 2 + n2)), reads=[self.r_ps[pb], self.r_colt],
                        writes=[r_v[gi * 2 + n2]])

            for gi in range(4):
                wnd = 2 << gi
                bufsets = []
                for kk in range(2):
                    m = gi * 2 + kk
                    pb = m % 2

                    def mm(e, m=m, pb=pb):
                        for k in range(8):
                            ins = e.matmul(self.ps[pb][:], w_in[:, k, m * 128:(m + 1) * 128], self.hs[:, k, :],
                                           start=(k == 0), stop=(k == 7))
                        return ins
                    S.op("pe", mm, reads=rW[0] + self.r_hs, writes=[self.r_ps[pb]])
                    u, ru = self.tmp[2 + kk], self.r_tmp[2 + kk]
                    if kk == 0:
                        bufs = [(self.tmp[4], self.r_tmp[4]), (self.tmp[5], self.r_tmp[5])]
                    else:
                        bufs = [(self.tmp[0], self.r_tmp[0]), (self.tmp[1], self.r_tmp[1])]
                    bufsets.append((m, pb, u, ru, bufs))
                if gi > 0:
                    emit_mmg(gi - 1)
                for (m, pb, u, ru, bufs) in bufsets:
                    S.op("dve", lambda e, u=u, m=m: e.tensor_copy(out=u[:, 0:HO], in_=self.halo[:, m, 0:HO]),
                         reads=[self.r_halo[m]], writes=[ru])
                for (m, pb, u, ru, bufs) in bufsets:
                    S.op("act", lambda e, u=u, pb=pb: e.activation(out=u[:, HO:HO + T], in_=self.ps[pb][:], func=AF.Identity),
                         reads=[self.r_ps[pb]], writes=[ru])
                for (m, pb, u, ru, bufs) in bufsets:
                    S.op("dve", lambda e, u=u, m=m: e.tensor_copy(out=self.halo[:, m, 0:HO], in_=u[:, T:T + HO]),
                         reads=[ru], writes=[self.r_halo[m]])
                state = []
                for (m, pb, u, ru, bufs) in bufsets:
                    state.append(dict(cur=u, rcur=ru, bi=0, lo=0))
                sh = 1
                while sh < wnd:
                    for kk, (m, pb, u, ru, bufs) in enumerate(bufsets):
                        stt = state[kk]
                        nb, rnb = bufs[stt["bi"]]
                        stt["bi"] ^= 1
                        lo2 = stt["lo"] + sh
                        cur, rcur = stt["cur"], stt["rcur"]
                        S.op("dve" if kk == 0 else "pool", lambda e, cur=cur, nb=nb, lo2=lo2, sh=sh: e.tensor_tensor(
                            out=nb[:, lo2:HO + T], in0=cur[:, lo2:HO + T], in1=cur[:, lo2 - sh:HO + T - sh], op=ALU.add),
                            reads=[rcur], writes=[rnb])
                        stt["cur"], stt["rcur"], stt["lo"] = nb, rnb, lo2
                    sh *= 2
                for kk, (m, pb, u, ru, bufs) in enumerate(bufsets):
                    stt = state[kk]
                    cur, rcur = stt["cur"], stt["rcur"]
                    S.op("dve", lambda e, cur=cur, u=u, kk=kk, wnd=wnd: e.scalar_tensor_tensor(
                        out=pbuf[kk][:], in0=cur[:, HO:HO + T], scalar=1.0 / wnd, in1=u[:, HO:HO + T],
                        op0=ALU.mult, op1=ALU.subtract), reads=[rcur, ru], writes=[self.r_sq[kk]])
                    if t % TPS == 0:
                        nb, rnb = bufs[stt["bi"]]
                        ic = self.colt[:, CL["invc"] + gi * 16:CL["invc"] + gi * 16 + 16]
                        S.op("dve", lambda e, cur=cur, nb=nb, ic=ic: e.tensor_tensor(
                            out=nb[:, 0:16], in0=cur[:, HO:HO + 16], in1=ic, op=ALU.mult),
                            reads=[rcur, self.r_colt], writes=[rnb])
                        S.op("dve", lambda e, nb=nb, u=u, kk=kk: e.tensor_tensor(
                            out=pbuf[kk][:, 0:16], in0=nb[:, 0:16], in1=u[:, HO:HO + 16], op=ALU.subtract),
                            reads=[rnb, ru], writes=[self.r_sq[kk]])
            emit_mmg(3)
            hook = None
            if t + 1 < NT:
                self.emit_norm_stats(t + 1, l, 0)
                def hook(m2, t=t):
                    if m2 < 4:
                        self.emit_norm_chunk(t + 1, l, 0, 2 * m2)
                        self.emit_norm_chunk(t + 1, l, 0, 2 * m2 + 1)
            self.emit_oproj(t, l, w_out, rW[2], v, r_v[0:8], 8, 2, (4, 5), hook)
            toks = self.store_x(dst, t)
        return toks

    def pass_qkv(self, pi, l, src):
        S = self.S
        reg = self.wreg_of[pi]
        oth = reg ^ 1
        w = self.wview(reg, 0, 8, 3 * D)
        rW = [[r, self.rG[reg]] for r in self.rW[reg]]
        qst, r_qst = self.g, self.r_g
        kst = self.W[oth][:, 8192:12288].rearrange("p (k n) -> p k n", k=8)
        vst = self.W[oth][:, 12288:16384].rearrange("p (k n) -> p k n", k=4)
        r_kst = [Res("kst%d" % i) for i in range(8)]
        r_vst = [Res("vst%d" % i) for i in range(4)]
        d_q, d_k, d_v = S.dsem(), S.dsem(), S.dsem()
        qv = self.qscr.rearrange("(m p) n -> p m n", p=128)
        kv = self.kscr.rearrange("(m p) n -> p m n", p=128)
        vv = self.vscr.rearrange("(n p) f -> p n f", p=128)
        mod_slots = []
        for q in range(2):
            mod_slots.append((self.W[oth][:, 16384 + q * 8192:16384 + (q + 1) * 8192].rearrange("p (k n) -> p k n", k=8),
                              Res("mstg%d" % q), S.dsem(), self.rG[oth]))
        mod_parts = [(l2, j2) for l2 in self.deferred_mods for j2 in range(6)]
        self.load_x(src, 0, None)
        self.emit_norm(0, l, 0)
        n = 0
        for t in range(NT):
            if t + 1 < NT:
                self.load_x(src, t + 1, None)
            items = [(which, hd) for which in range(2) for hd in range(8)]

            def post(n_, which, hd):
                pq = n_ % 4
                pss = 4 + n_ % 2
                st_, r_st = (qst, r_qst) if which == 0 else (kst, r_kst)
                gcol = self.col("qn") if which == 0 else self.col("kn")
                sq, rsq = self.sq[n_ % 2], self.r_sq[n_ % 2]
                S.op("pe", lambda e: e.matmul(self.ps[pss][:], self.bones[:], sq[:], start=True, stop=True),
                     reads=[rsq, self.r_ones], writes=[self.r_ps[pss]])
                sdt, rsdt = self.tmp[2 + n_ % 2], self.r_tmp[2 + n_ % 2]
                if which == 0:
                    S.op("act", lambda e: e.activation(
                        out=sdt[:, 0:T], in_=self.ps[pss][:], func=AF.Ln, bias=self.lamc[:, 3:4], scale=1.0),
                        reads=[self.r_ps[pss], self.r_lam], writes=[rsdt])
                else:
                    S.op("act", lambda e: e.activation(
                        out=sdt[:, 0:T], in_=self.ps[pss][:], func=AF.Ln, bias=self.epsc[:], scale=1.0 / 64),
                        reads=[self.r_ps[pss], self.r_ones], writes=[rsdt])
                S.op("act", lambda e: e.activation(out=sdt[:, 0:T], in_=sdt[:, 0:T], func=AF.Exp, scale=-0.5),
                     reads=[rsdt], writes=[rsdt])
                S.op("dve", lambda e: e.scalar_tensor_tensor(
                    out=st_[:, hd, :], in0=self.ps[pq][:], scalar=gcol, in1=sdt[:, 0:T], op0=ALU.mult, op1=ALU.mult),
                    reads=[self.r_ps[pq], rsdt, self.r_colt], writes=[r_st[hd]])

            prev = None
            for (which, hd) in items:
                pq = n % 4

                def mm(e, which=which, hd=hd, pq=pq):
                    c0 = which * D + hd * 128
                    for k in range(8):
                        ins = e.matmul(self.ps[pq][:], w[:, k, c0:c0 + 128], self.hs[:, k, :],
                                       start=(k == 0), stop=(k == 7))
                    return ins
                S.op("pe", mm, reads=rW[0] + self.r_hs, writes=[self.r_ps[pq]])
                sq, rsq = self.sq[n % 2], self.r_sq[n % 2]
                S.op("act", lambda e, pq=pq, sq=sq: e.activation(out=sq[:], in_=self.ps[pq][:], func=AF.Square),
                     reads=[self.r_ps[pq]], writes=[rsq])
                if prev is not None:
                    post(*prev)
                prev = (n, which, hd)
                n += 1
            post(*prev)
            for jb in range(4):
                for nh in range(2):
                    pv = (jb * 2 + nh) % 4

                    def mmv(e, jb=jb, nh=nh, pv=pv):
                        for k in range(8):
                            ins = e.matmul(self.ps[pv][:], self.hs[:, k, jb * 128:(jb + 1) * 128],
                                           w[:, k, 2 * D + nh * 512:2 * D + (nh + 1) * 512], start=(k == 0), stop=(k == 7))
                        return ins
                    S.op("pe", mmv, reads=rW[0] + self.r_hs, writes=[self.r_ps[pv]])
                    S.op("dve", lambda e, jb=jb, nh=nh, pv=pv: e.tensor_copy(
                        out=vst[:, jb, nh * 512:(nh + 1) * 512], in_=self.ps[pv][:]),
                        reads=[self.r_ps[pv]], writes=[r_vst[jb]])
            S.dma("sp", lambda e, t=t: e.dma_start(out=qv[:, :, self.xcols(t)], in_=qst[:, 0:8, :]), d_q,
                  reads=r_qst[0:8], writes=[self.dres("q", t)])
            S.dma("sp", lambda e, t=t: e.dma_start(out=kv[:, :, self.xcols(t)], in_=kst), d_k,
                  reads=r_kst, writes=[self.dres("k", t)])
            S.dma("sp", lambda e, t=t: e.dma_start(out=vv[:, t * 4:(t + 1) * 4, :], in_=vst), d_v,
                  reads=r_vst, writes=[self.dres("v", t)])
            if t < len(mod_parts):
                self.emit_mod_part(mod_parts[t][0], mod_parts[t][1], mod_slots[t % 2])
            if t + 1 < NT:
                self.emit_norm(t + 1, l, 0, ring=(0, 1, 4, 5))
        assert len(mod_parts) <= NT
        self.emit_mod_finish(self.deferred_mods)
        return []

    def pass_att(self, pi, l):
        S = self.S
        reg = self.wreg_of[pi - 1]
        oth = reg ^ 1
        lam_init = 0.8 - 0.6 * math.exp(-0.3 * l)
        WR = self.W[reg]
        sets = []
        for i in range(2):
            base = i * 12288
            sets.append(dict(
                K=WR[:, base:base + 4096], Q=WR[:, base + 4096:base + 8192],
                V=WR[:, base + 8192:base + 12288].rearrange("p (n f) -> p n f", n=32),
                rK=Res("K%d" % i), rQ=Res("Q%d" % i), rV=Res("V%d" % i),
                dK=S.dsem(), dQ=S.dsem(), dV=S.dsem()))
        for i in range(2):
            for nm in ("rK", "rQ", "rV"):
                pass
        NE = 4
        et = [WR[:, 24576 + i * 1024:24576 + (i + 1) * 1024].rearrange("p (m n) -> p m n", m=2) for i in range(NE)]
        r_et = [Res("e%d" % i) for i in range(NE)]
        masks = self.W[oth][:, 16384:32768]
        r_mask = Res("mask")
        d_mask = S.dsem()
        d_o = S.dsem()
        ov = self.oscr
        vv = self.vscr.rearrange("(n p) f -> p n f", p=128)
        all_prev = self.rW[reg] + self.rW[oth]
        S.dma("sp", lambda e: e.dma_start(out=masks, in_=self.amask), d_mask, reads=[], writes=[r_mask, self.rG[oth]])
        pl = self.ps[0]
        S.op("dve", lambda e: e.tensor_tensor(out=self.tmp[0][:, 0:1], in0=self.col("lam", 0), in1=self.col("lam", 1), op=ALU.mult),
             reads=[self.r_colt], writes=[self.r_tmp[0]])
        S.op("dve", lambda e: e.tensor_tensor(out=self.tmp[0][:, 1:2], in0=self.col("lam", 2), in1=self.col("lam", 3), op=ALU.mult),
             reads=[self.r_colt], writes=[self.r_tmp[0]])
        S.op("pe", lambda e: e.matmul(pl[:, 0:2], self.onesf[:], self.tmp[0][:, 0:2], start=True, stop=True),
             reads=[self.r_tmp[0], self.r_ones], writes=[self.r_ps[0]])
        S.op("act", lambda e: e.activation(out=self.tmp[1][:, 0:2], in_=pl[:, 0:2], func=AF.Exp),
             reads=[self.r_ps[0]], writes=[self.r_tmp[1]])
        S.op("dve", lambda e: e.scalar_tensor_tensor(out=self.lamc[:, 0:1], in0=self.tmp[1][:, 0:1], scalar=lam_init,
                                                      in1=self.tmp[1][:, 1:2], op0=ALU.add, op1=ALU.subtract),
             reads=[self.r_tmp[1]], writes=[self.r_lam])
        S.op("dve", lambda e: e.tensor_scalar(out=self.lamc[:, 1:2], in0=self.lamc[:, 0:1], scalar1=-1.0, scalar2=None, op0=ALU.mult),
             reads=[self.r_lam], writes=[self.r_lam])
        S.op("dve", lambda e: e.tensor_scalar(out=self.lamc[:, 2:3], in0=self.col("subln"), scalar1=1.0 - lam_init, scalar2=None,
                                              op0=ALU.mult), reads=[self.r_colt], writes=[self.r_lam])

        heads = [(s_, hd) for s_ in range(NSEQ) for hd in range(8)]

        def load_head(i):
            s_, hd = heads[i]
            st_ = sets[i % 2]
            cs = slice(s_ * SEQ, (s_ + 1) * SEQ)
            rd_k = [self.dres("k", t) for t in range(s_ * TPS, (s_ + 1) * TPS)]
            rd_q = [self.dres("q", t) for t in range(s_ * TPS, (s_ + 1) * TPS)]
            rd_v = [self.dres("v", t) for t in range(s_ * TPS, (s_ + 1) * TPS)]
            extra = [self.rG[reg]] if i < 2 else []
            S.dma("sp", lambda e: e.dma_start(out=st_["K"], in_=self.kscr[hd * 128:(hd + 1) * 128, cs]), st_["dK"],
                  reads=rd_k, writes=[st_["rK"]] + extra)
            S.dma("sp", lambda e: e.dma_start(out=st_["Q"], in_=self.qscr[hd * 128:(hd + 1) * 128, cs]), st_["dQ"],
                  reads=rd_q, writes=[st_["rQ"]])
            for n0 in range(0, 32, 8):
                S.dma("sp", lambda e, n0=n0: e.dma_start(out=st_["V"][:, n0:n0 + 8, :],
                                                         in_=vv[:, s_ * 32 + n0:s_ * 32 + n0 + 8, hd * 128:(hd + 1) * 128]),
                      st_["dV"], reads=rd_v, writes=[st_["rV"]])

        steps = []
        for i, (s_, hd) in enumerate(heads):
            for qt in range(TPS):
                nkb = 4 * qt + 4
                for kb in range(nkb):
                    steps.append((i, s_, hd, qt, kb, nkb))
        ecnt = [0]

        def emit_score(step, slot):
            i, s_, hd, qt, kb, nkb = step
            st_ = sets[i % 2]
            j = kb - 4 * qt
            c0 = 128 * j if j > 0 else 0
            d0 = 512 * qt - 128 * kb
            sl = slot % 2
            for m in range(2):
                pb = sl * 2 + m
                S.op("pe", lambda e, m=m, pb=pb: e.matmul(
                    self.ps[pb][:, c0:T], st_["K"][m * 64:(m + 1) * 64, kb * 128:(kb + 1) * 128],
                    st_["Q"][m * 64:(m + 1) * 64, qt * T + c0:(qt + 1) * T], start=True, stop=True),
                    reads=[st_["rK"], st_["rQ"], self.rG[reg]], writes=[self.r_ps[pb]])
            ei = slot % NE
            ebuf, reb = et[ei], r_et[ei]
            for (a0, a1, cq) in att_centers(hd):
                lo = max(a0, c0)
                if lo >= a1:
                    continue
                bc = self.col("abias", ABI[(hd, d0 + cq)])
                S.op("act", lambda e, sl=sl, ebuf=ebuf, lo=lo, a1=a1, bc=bc: e.activation(
                    out=ebuf[:, :, lo:a1], in_=self.psall[:, 2 * sl:2 * sl + 2, lo:a1], func=AF.Exp, bias=bc, scale=1.0),
                    reads=[self.r_ps[2 * sl], self.r_ps[2 * sl + 1], self.r_colt], writes=[reb])
            if j >= 0:
                mk = masks[:, (hd * 4 + j) * 512:(hd * 4 + j + 1) * 512]
                for m in range(2):
                    S.op("dve", lambda e, ebuf=ebuf, mk=mk, m=m: e.tensor_tensor(
                        out=ebuf[:, m, c0:T], in0=ebuf[:, m, c0:T], in1=mk[:, c0:T], op=ALU.mult),
                        reads=[reb, r_mask], writes=[reb])

        def emit_pv(step, slot):
            i, s_, hd, qt, kb, nkb = step
            st_ = sets[i % 2]
            j = kb - 4 * qt
            c0 = 128 * j if j > 0 else 0
            ei = slot % NE
            ebuf, reb = et[ei], r_et[ei]
            for m in range(2):
                S.op("pe", lambda e, m=m, ebuf=ebuf: e.matmul(
                    self.ps[4 + m][:, c0:T], st_["V"][:, kb, :], ebuf[:, m, c0:T], start=(kb == 0), stop=(kb == nkb - 1)),
                    reads=[st_["rV"], reb, self.rG[reg], self.rG[oth]], writes=[self.r_ps[4 + m]])
                S.op("pe", lambda e, m=m, ebuf=ebuf: e.matmul(
                    self.ps[6 + m][:, c0:T], self.ones[:], ebuf[:, m, c0:T], start=(kb == 0), stop=(kb == nkb - 1)),
                    reads=[self.r_ones, reb], writes=[self.r_ps[6 + m]])

        def emit_final(step):
            i, s_, hd, qt, kb, nkb = step
            rt = self.r_tmp
            tA = self.tmpall
            S.op("dve", lambda e: e.tensor_copy(out=tA[:, 0:2, 0:T], in_=self.psall[:, 4:6, :]),
                 reads=[self.r_ps[4], self.r_ps[5]], writes=[rt[0], rt[1]])
            S.op("act", lambda e: e.activation(out=tA[:, 2:4, 0:T], in_=self.psall[:, 6:8, :], func=AF.Identity),
                 reads=[self.r_ps[6], self.r_ps[7]], writes=[rt[2], rt[3]])
            for m in range(2):
                S.op("dve", lambda e, m=m: e.reciprocal(out=tA[:, 2 + m, 0:T], in_=tA[:, 2 + m, 0:T]),
                     reads=[rt[2 + m]], writes=[rt[2 + m]])
            for m in range(2):
                S.op("dve", lambda e, m=m: e.tensor_tensor(out=tA[:, m, 0:T], in0=tA[:, m, 0:T], in1=tA[:, 2 + m, 0:T], op=ALU.mult),
                     reads=[rt[m], rt[2 + m]], writes=[rt[m]])
            q = ecnt[0] % 2
            ecnt[0] += 1
            C, rC = self.tmp[4 + q], rt[4 + q]
            S.op("dve", lambda e: e.scalar_tensor_tensor(out=C[:, 0:T], in0=tA[:, 1, 0:T], scalar=self.lamc[:, 1:2], in1=tA[:, 0, 0:T],
                                                          op0=ALU.mult, op1=ALU.add), reads=[rt[0], rt[1], self.r_lam], writes=[rC])
            sq, rsq = self.sq[q], self.r_sq[q]

            def part2():
                S.op("act", lambda e: e.activation(out=sq[:], in_=C[:, 0:T], func=AF.Square), reads=[rC], writes=[rsq])
                S.op("pe", lambda e: e.matmul(self.ps[0][:], self.ones[:], sq[:], start=True, stop=True),
                     reads=[rsq, self.r_ones], writes=[self.r_ps[0]])
                S.op("act", lambda e: e.activation(out=self.rstd[:], in_=self.ps[0][:], func=AF.Ln, bias=self.epsc[:], scale=1.0 / 128),
                     reads=[self.r_ps[0], self.r_ones], writes=[self.r_rstd])
                S.op("act", lambda e: e.activation(out=self.rstd[:], in_=self.rstd[:], func=AF.Exp, scale=-0.5),
                     reads=[self.r_rstd], writes=[self.r_rstd])
                S.op("dve", lambda e: e.scalar_tensor_tensor(out=sq[:], in0=C[:, 0:T], scalar=self.lamc[:, 2:3], in1=self.rstd[:],
                                                              op0=ALU.mult, op1=ALU.mult), reads=[rC, self.r_rstd, self.r_lam], writes=[rsq])
                tg = s_ * TPS + qt
                S.dma("sp", lambda e: e.dma_start(out=ov[hd * 128:(hd + 1) * 128, tg * T:(tg + 1) * T], in_=sq[:]), d_o,
                      reads=[rsq], writes=[self.dres("o%d" % hd, tg)])
            return part2

        load_head(0)
        LAG = 2
        DEFER = 8
        pending = []
        for si in range(len(steps) + LAG):
            if si < len(steps):
                emit_score(steps[si], si)
            pj = si - LAG
            if pj >= 0:
                i, s_, hd, qt, kb, nkb = steps[pj]
                if qt == 0 and kb == 0 and i + 1 < len(heads):
                    load_head(i + 1)
                emit_pv(steps[pj], pj)
                if steps[pj][4] == steps[pj][5] - 1:
                    pending.append((pj + DEFER, emit_final(steps[pj])))
                while pending and pending[0][0] <= pj:
                    pending.pop(0)[1]()
        for _, fn in pending:
            fn()
        return []

    def pass_oproj(self, pi, l, src, dst):
        S = self.S
        reg = self.wreg_of[pi]
        w_out = self.wview(reg, 0, 8, D)
        rW = [[r, self.rG[reg]] for r in self.rW[reg]]
        ov = self.oscr.rearrange("(m p) n -> p m n", p=128)
        obuf = [(self.hs, self.r_hs, self.d_hs), (self.g, self.r_g[0:8], S.dsem())]
        toks = []

        def load_o(t):
            ob, rob, dob = obuf[t % 2]
            S.dma("sp", lambda e, t=t: e.dma_start(out=ob[:, 0:8, :], in_=ov[:, :, self.xcols(t)]), dob,
                  reads=[self.dres("o%d" % hd, t) for hd in range(8)], writes=rob)

        self.load_x(src, 0, None)
        load_o(0)
        for t in range(NT):
            if t + 1 < NT:
                self.load_x(src, t + 1, None)
                load_o(t + 1)
            ob, rob, dob = obuf[t % 2]
            self.emit_oproj(t, l, w_out, rW[0], ob, rob, 8, 2, (4, 5))
            toks = self.store_x(dst, t)
        return toks


def build_cols(inp):
    cols = np.zeros((128, CL.n), np.float32)

    def put(name, arr, idx=0):
        a = np.asarray(arr, np.float32)
        cols[:, CL[name] + idx:CL[name] + idx + a.shape[1]] = a
    for l in range(DEPTH):
        put("adab", vec_cols(inp["ada_b"][l]), l * 48)
        put("nmix", vec_cols(inp["norm_mix"][l]), l * 8)
        put("nffn", vec_cols(inp["norm_ffn"][l]), l * 8)
    for j in range(2):
        for k in range(3):
            put("convw", vec_cols(inp["conv_w"][j, k]), (j * 3 + k) * 8)
    put("pscale", vec_cols(inp["pool_scale"][0]))
    put("qn", np.concatenate([inp["diff_q_norm"][0]] * 2)[:, None])
    put("kn", np.concatenate([inp["diff_k_norm"][0]] * 2)[:, None])
    put("subln", np.asarray(inp["diff_subln"][0])[:, None])
    lam = np.zeros((128, 4), np.float32)
    for i, nm in enumerate(("diff_lq1", "diff_lk1", "diff_lq2", "diff_lk2")):
        lam[:64, i] = inp[nm][0]
    put("lam", lam)
    invc = np.zeros((128, 64), np.float32)
    for gi in range(4):
        w = 2 << gi
        for tt in range(16):
            invc[:, gi * 16 + tt] = 1.0 / min(tt + 1, w)
    put("invc", invc)
    ab = np.zeros((128, NAB), np.float32)
    kl = np.arange(128, dtype=np.float64)
    for (h, dlt), ci in ABI.items():
        ab[:, ci] = (2.0 ** (-(h + 1))) * (kl - dlt)
    put("abias", ab)
    return cols


def build_amask():
    m = np.zeros((128, 32, 512), np.float64)
    kl = np.arange(128)[:, None]
    ql = np.arange(512)[None, :]
    for h in range(8):
        slope = 2.0 ** (-(h + 1))
        for j in range(4):
            kp = 128 * j + kl
            allowed = (kp // 64) <= (ql // 64)
            fix = np.where(kp > ql, np.exp(-2.0 * slope * (kp - ql)), 1.0)
            m[:, h * 4 + j, :] = np.where(allowed, fix, 0.0)
    return m.reshape(128, 32 * 512).astype(np.float32).astype(mybir.dt.np(BF16))


_CACHE = {}


def get_nc(layers=(0, 1, 2, 3), stop_after=None):
    key = (tuple(layers), stop_after)
    if key not in _CACHE:
        _CACHE[key] = Builder(layers, stop_after).build()
    return _CACHE[key]


def make_in_maps(inp, ncores=NCORES):
    cols = build_cols(inp)
    amask = build_amask()
    shared = {
        "cols": cols, "amask": amask,
        "ada_w": np.ascontiguousarray(inp["ada_w"], np.float32),
        "ffn_w1": np.ascontiguousarray(inp["ffn_w1"], np.float32),
        "ffn_w3": np.ascontiguousarray(inp["ffn_w3"], np.float32),
        "ffn_w2": np.ascontiguousarray(inp["ffn_w2"], np.float32),
        "conv_w_in": np.ascontiguousarray(inp["conv_w_in"], np.float32),
        "conv_w_out": np.ascontiguousarray(inp["conv_w_out"], np.float32),
        "diff_w_qkv": np.ascontiguousarray(inp["diff_w_qkv"], np.float32),
        "diff_w_out": np.ascontiguousarray(inp["diff_w_out"], np.float32),
        "pool_w_in": np.ascontiguousarray(inp["pool_w_in"], np.float32),
        "pool_w_group": np.ascontiguousarray(inp["pool_w_group"], np.float32),
        "pool_w_out": np.ascontiguousarray(inp["pool_w_out"], np.float32),
    }
    x = np.asarray(inp["x"], np.float32)
    c = np.asarray(inp["c"], np.float32)
    maps = []
    for ci in range(ncores):
        xc = x[ci * NSEQ:(ci + 1) * NSEQ].reshape(NTOK, D)
        m = dict(shared)
        m["xin"] = np.ascontiguousarray(xc.T)
        cc = c[ci * NSEQ:(ci + 1) * NSEQ]
        m["ccol"] = np.ascontiguousarray(cc.reshape(NSEQ, 8, 128).transpose(2, 1, 0).reshape(128, 16))
        maps.append(m)
    return maps


def kernel(**inputs):
    nc = get_nc()
    maps = make_in_maps(inputs)
    res = run_bass_kernel_spmd(nc, maps, core_ids=list(range(NCORES)))
    out = np.empty((NCORES * NSEQ, SEQ, D), np.float32)
    for ci in range(NCORES):
        o = res.results[ci]["out"]
        out[ci * NSEQ:(ci + 1) * NSEQ] = o.T.reshape(NSEQ, SEQ, D)
    return out
```
